# Optimizing a Trainium2 kernel written in Bass

```python
import jax, jax.numpy as jnp
from jax import lax
import numpy as np

D_MODEL = 1024
BATCH = 8
SEQ = 2048
DEPTH = 1
DEC_BATCH = 128
DEC_SEQ = 4
PAST_LEN = 16384
PAGE_SIZE = 128

CHUNK = 128
H_A = 8
W_A = D_MODEL
DH_A = W_A // H_A
H_B = 8
W_B = D_MODEL
CONV_W = 3
P_DIM = 256
EPS = 1e-6
LN_EPS = 1e-5
SPLIT_WIDTHS = [W_A, W_A, W_A, W_B, W_B, W_B, W_B, D_MODEL, D_MODEL]
SPLIT_IDX = [int(i) for i in np.cumsum(SPLIT_WIDTHS)[:-1]]
IN_COLS = int(sum(SPLIT_WIDTHS))

kernel_name = "gated_parallel_gmlp_shortconv_decoder_step"


def rms_norm(x, g):
    xf = x.astype(jnp.float32)
    y = xf * lax.rsqrt(jnp.mean(xf * xf, axis=-1, keepdims=True) + EPS)
    return (y * g.astype(jnp.float32)).astype(x.dtype)


def layer_norm(x, g, b):
    xf = x.astype(jnp.float32)
    mu = jnp.mean(xf, axis=-1, keepdims=True)
    var = jnp.mean(jnp.square(xf - mu), axis=-1, keepdims=True)
    y = (xf - mu) * lax.rsqrt(var + LN_EPS)
    return (y * g.astype(jnp.float32) + b.astype(jnp.float32)).astype(x.dtype)


def chunk_spatial_mix(v, w_s, b_s):
    bsz, L, _ = v.shape
    n = -(-L // CHUNK)
    pad = n * CHUNK - L
    vp = jnp.pad(v, ((0, 0), (0, pad), (0, 0))).reshape(bsz, n, CHUNK, H_A, DH_A)
    mask = jnp.tril(jnp.ones((CHUNK, CHUNK), dtype=bool))
    w = jnp.where(mask[None], w_s, jnp.zeros_like(w_s))
    s = jnp.einsum('hts,bnshd->bnthd', w, vp) + jnp.transpose(b_s)[None, None, :, :, None]
    return s.reshape(bsz, n * CHUNK, W_A)[:, :L]


def causal_conv(u, state, conv_w):
    L = u.shape[1]
    full = jnp.concatenate([state.astype(u.dtype), u], axis=1)
    y = conv_w[0] * full[:, 0:L]
    for k in range(1, CONV_W):
        y = y + conv_w[k] * full[:, k:k + L]
    return y, full[:, -(CONV_W - 1):]


def mixer_layer(x, p, conv_state, norm_g, w_in, ln_v_g, ln_v_b, w_s, b_s, conv_w,
                w_a_out, w_b_out, w_o, pe_norm_g, w_pe_gate, w_pe_proj):
    h = rms_norm(x, norm_g)
    z = jnp.einsum('bld,de->ble', h, w_in)
    u_a, v_a, gate_a, c_b, b_b, h_b, gate_b, m_a, m_b = jnp.split(z, SPLIT_IDX, axis=-1)
    u_a = jax.nn.gelu(u_a)
    v_a = layer_norm(jax.nn.gelu(v_a), ln_v_g, ln_v_b)
    s = chunk_spatial_mix(v_a, w_s, b_s)
    y_a = u_a * s * jax.nn.silu(gate_a)
    conv_out, new_conv = causal_conv(c_b * h_b, conv_state, conv_w)
    y_b = b_b * conv_out * jax.nn.silu(gate_b)
    merged = (jax.nn.sigmoid(m_a) * jnp.einsum('blw,wd->bld', y_a, w_a_out)
              + jax.nn.sigmoid(m_b) * jnp.einsum('blw,wd->bld', y_b, w_b_out))
    x = x + jnp.einsum('bld,de->ble', merged, w_o)
    pe_gate = jax.nn.sigmoid(jnp.einsum('bld,de->ble', rms_norm(x, pe_norm_g), w_pe_gate))
    x = x + pe_gate * jnp.einsum('blp,pd->bld', p, w_pe_proj)
    L = v_a.shape[1]
    start = ((L - 1) // CHUNK) * CHUNK
    return x, new_conv, v_a[:, start:]


def setup_inputs(seed: int = 0) -> dict:
    key = jax.random.key(seed)
    ks = jax.random.split(key, 20)
    f32 = jnp.float32
    nrm = lambda k, shape, scale: jax.random.normal(k, shape, f32) * scale
    return {
        "x_prompt": nrm(ks[0], (BATCH, SEQ, D_MODEL), 1.0),
        "x_sample": nrm(ks[1], (DEC_BATCH, DEC_SEQ, D_MODEL), 1.0),
        "state_conv": nrm(ks[2], (DEPTH, DEC_BATCH, CONV_W - 1, W_B), 0.5),
        "p_prompt": nrm(ks[3], (DEPTH, BATCH, SEQ, P_DIM), 1.0),
        "p_sample": nrm(ks[4], (DEPTH, DEC_BATCH, DEC_SEQ, P_DIM), 1.0),
        "norm_g": 1.0 + nrm(ks[5], (DEPTH, D_MODEL), 0.1),
        "w_in": nrm(ks[6], (DEPTH, D_MODEL, IN_COLS), D_MODEL ** -0.5),
        "ln_v_g": 1.0 + nrm(ks[7], (DEPTH, W_A), 0.1),
        "ln_v_b": nrm(ks[8], (DEPTH, W_A), 0.02),
        "w_s": nrm(ks[9], (DEPTH, H_A, CHUNK, CHUNK), 0.5 * CHUNK ** -0.5),
        "b_s": 1.0 + nrm(ks[10], (DEPTH, H_A, CHUNK), 0.1),
        "conv_w": nrm(ks[11], (DEPTH, CONV_W, W_B), CONV_W ** -0.5),
        "w_a_out": nrm(ks[12], (DEPTH, W_A, D_MODEL), W_A ** -0.5),
        "w_b_out": nrm(ks[13], (DEPTH, W_B, D_MODEL), W_B ** -0.5),
        "w_o": nrm(ks[14], (DEPTH, D_MODEL, D_MODEL), D_MODEL ** -0.5),
        "pe_norm_g": 1.0 + nrm(ks[15], (DEPTH, D_MODEL), 0.1),
        "w_pe_gate": nrm(ks[16], (DEPTH, D_MODEL, D_MODEL), D_MODEL ** -0.5),
        "w_pe_proj": nrm(ks[17], (DEPTH, P_DIM, D_MODEL), P_DIM ** -0.5),
        "final_norm_g": 1.0 + nrm(ks[18], (D_MODEL,), 0.1),
    }


def reference(x_prompt, x_sample, state_conv, p_prompt, p_sample, norm_g, w_in, ln_v_g,
              ln_v_b, w_s, b_s, conv_w, w_a_out, w_b_out, w_o, pe_norm_g, w_pe_gate,
              w_pe_proj, final_norm_g):
    xp, xs = x_prompt, x_sample
    conv_p_list, conv_s_list, v_p_list, v_s_list = [], [], [], []
    zero_state = jnp.zeros((x_prompt.shape[0], CONV_W - 1, W_B), x_prompt.dtype)
    for i in range(DEPTH):
        params = (norm_g[i], w_in[i], ln_v_g[i], ln_v_b[i], w_s[i], b_s[i], conv_w[i],
                  w_a_out[i], w_b_out[i], w_o[i], pe_norm_g[i], w_pe_gate[i], w_pe_proj[i])
        xp, cp, vp = mixer_layer(xp, p_prompt[i], zero_state, *params)
        xs, cs, vs = mixer_layer(xs, p_sample[i], state_conv[i], *params)
        conv_p_list.append(cp)
        conv_s_list.append(cs)
        v_p_list.append(vp)
        v_s_list.append(vs)
    y_prompt = rms_norm(xp, final_norm_g)
    y_sample = rms_norm(xs, final_norm_g)
    conv_state_prompt = jnp.stack(conv_p_list)
    conv_state_sample = jnp.stack(conv_s_list)
    v_rows_prompt = jnp.stack(v_p_list)
    v_rows_sample = jnp.stack(v_s_list)
    return (y_prompt, y_sample, conv_state_prompt, conv_state_sample, v_rows_prompt, v_rows_sample)
```

```python
import numpy as np
import os
import concourse.bass as bass
import concourse.mybir as mybir
from concourse.bass_utils import run_bass_kernel_spmd
from contextlib import ExitStack

F32 = mybir.dt.float32
BF16 = mybir.dt.bfloat16
AF = mybir.ActivationFunctionType
ALU = mybir.AluOpType

D = 1024
SEQ = 2048
NS = 64
NT = SEQ + NS
NCH = 17
P_DIM = 256
EPS = 1e-6
LN_EPS = 1e-5
GROUPS = [(0, 512), (512, 512), (1024, 512), (1536, 288), (1824, 288)]
NRING = 8
N_CORES = 8


def crow(c):
    return 128 if c < 16 else 64


def group_chunks(g):
    t0, n = GROUPS[g]
    out = []
    for c in range(NCH):
        a, b = max(t0, c * 128), min(t0 + n, c * 128 + crow(c))
        if a < b:
            out.append((c, a - c * 128, b - c * 128, a - t0))
    return out


class Buf:
    __slots__ = ("name", "w", "r", "dsem", "dcnt")

    def __init__(self, name):
        self.name = name
        self.w = {}
        self.r = {}
        self.dsem = None
        self.dcnt = 0

    def inherit(self, *others):
        for o in others:
            for d in (o.w, o.r):
                for k, (s, v) in d.items():
                    if k not in self.r or self.r[k][1] < v:
                        self.r[k] = (s, v)


def _merge(d, tok):
    k = id(tok[0])
    if k not in d or d[k][1] < tok[1]:
        d[k] = tok


class Sched:
    CE = ("pe", "act", "dve", "pool")

    def __init__(self, nc, stack):
        self.nc = nc
        self.stack = stack
        self.q = {e: [] for e in ("pe", "act", "dve", "pool", "sp")}
        self.sem = {e: stack.enter_context(nc.semaphore("s_" + e)) for e in self.CE}
        self.cnt = {e: 0 for e in self.CE}
        self.waited = {e: {} for e in self.q}
        self.nsem = 4

    def new_sem(self, name):
        self.nsem += 1
        return self.stack.enter_context(self.nc.semaphore(name))

    def _waits(self, eng, reads, writes):
        need = {}
        for b in reads:
            for tok in b.w.values():
                _merge(need, tok)
        for b in writes:
            for tok in b.w.values():
                _merge(need, tok)
            for tok in b.r.values():
                _merge(need, tok)
        wd = self.waited[eng]
        for k, (s, v) in need.items():
            if wd.get(k, 0) >= v:
                continue
            wd[k] = v
            self.q[eng].append(("wait", s, v))

    def _commit(self, tok, reads, writes):
        for b in writes:
            b.w = {id(tok[0]): tok}
            b.r = {}
        for b in reads:
            _merge(b.r, tok)

    def op(self, eng, fn, reads=(), writes=()):
        return self.group(eng, [fn], reads, writes)

    def group(self, eng, fns, reads=(), writes=()):
        self._waits(eng, reads, writes)
        self.cnt[eng] += 1
        tok = (self.sem[eng], self.cnt[eng])
        for f in fns[:-1]:
            self.q[eng].append(("op", f, None))
        self.q[eng].append(("op", fns[-1], (self.sem[eng], 1)))
        self._commit(tok, reads, writes)
        return tok

    def dma(self, eng, fns, owner, reads=(), writes=()):
        if owner.dsem is None:
            owner.dsem = self.new_sem("d_" + owner.name)
        self._waits(eng, reads, writes)
        for f in fns:
            owner.dcnt += 16
            self.q[eng].append(("op", f, (owner.dsem, 16)))
        tok = (owner.dsem, owner.dcnt)
        self._commit(tok, reads, writes)
        _merge(owner.r, tok)
        return tok

    def wait_all(self, eng, bufs):
        self._waits(eng, (), bufs)

    def emit(self, block):
        nc = self.nc
        table = {"pe": block.tensor, "act": block.scalar, "dve": block.vector, "pool": block.gpsimd, "sp": block.sync}
        for name, deco in table.items():
            items = self.q[name]

            def body(e, items=items):
                for it in items:
                    if it[0] == "wait":
                        e.wait_ge(it[1], it[2])
                    else:
                        ins = it[1](e)
                        if it[2] is not None:
                            ins.then_inc(it[2][0], it[2][1])
            deco(body)


def MM(out, lhsT, rhs, start, stop):
    return lambda e: e.matmul(out, lhsT=lhsT, rhs=rhs, start=start, stop=stop)


def TR(out, in_, idn):
    return lambda e: e.transpose(out, in_, idn)


def ACT(out, in_, func, **kw):
    return lambda e: e.activation(out=out, in_=in_, func=func, **kw)


def TT(out, in0, in1, op):
    return lambda e: e.tensor_tensor(out=out, in0=in0, in1=in1, op=op)


def STT(out, in0, scalar, in1, op0, op1):
    return lambda e: e.scalar_tensor_tensor(out=out, in0=in0, scalar=scalar, in1=in1, op0=op0, op1=op1)


def TS(out, in0, s1, s2, op0, op1=None):
    if op1 is None:
        return lambda e: e.tensor_scalar(out=out, in0=in0, scalar1=s1, scalar2=None, op0=op0)
    return lambda e: e.tensor_scalar(out=out, in0=in0, scalar1=s1, scalar2=s2, op0=op0, op1=op1)


def CP(out, in_):
    return lambda e: e.tensor_copy(out=out, in_=in_)


def MS(ap, val):
    return lambda e: e.memset(ap, val)


def DMA(out, in_):
    return lambda e: e.dma_start(out=out, in_=in_)

def build_program(stop_after=99, dbg=()):
    nc = bass.Bass("TRN2", target_bir_lowering=False)

    def din(name, shape):
        return nc.dram_tensor(name, list(shape), F32, kind="ExternalInput").ap()

    def dout(name, shape):
        return nc.dram_tensor(name, list(shape), F32, kind="ExternalOutput").ap()

    x_d = din("x", [NT, D])
    p_d = din("p", [NT, P_DIM])
    st_d = din("st", [32, D])
    w_in_d = din("w_in", [D, 9 * D])
    w_a_d = din("w_a", [D, D])
    w_b_d = din("w_b", [D, D])
    w_o_d = din("w_o", [D, D])
    w_pg_d = din("w_pg", [D, D])
    w_pp_d = din("w_pp", [P_DIM, D])
    norm_g_d = din("norm_g", [D])
    ln_g_d = din("ln_g", [D])
    ln_b_d = din("ln_b", [D])
    pe_g_d = din("pe_g", [D])
    fin_g_d = din("fin_g", [D])
    w_s_d = din("w_s", [8, 128, 128])
    b_s_d = din("b_s", [8, 128])
    conv_w_d = din("conv_w", [3, D])
    y_d = dout("y", [NT, D])
    csp_d = dout("csp", [2, D])
    css_d = dout("css", [32, D])
    vp_d = dout("vp", [128, D])
    vs_d = dout("vs", [64, D])
    dbg_d = {}
    for name, shape, dt in dbg:
        dbg_d[name] = nc.dram_tensor(name, list(shape), dt, kind="ExternalOutput").ap()

    with ExitStack() as es:
        def sb(name, shape, dt):
            return es.enter_context(nc.sbuf_tensor(name, list(shape), dt))

        HT = sb("HT", [128, 8 * NT], BF16)
        VA = sb("VA", [128, 17 * D], BF16)
        YA = sb("YA", [128, 8 * NT], BF16)
        MG = sb("MG", [128, 8 * NT], BF16)
        WR = sb("WR", [128, NRING * 2048], BF16)
        WPP = sb("WPP", [128, 2 * D], BF16)
        SCR = sb("SCR", [128, 4224], F32)
        ident = sb("ident", [128, 128], BF16)
        identf = sb("identf", [128, 128], F32)
        maskf = sb("maskf", [128, 128], F32)
        wT = sb("wT", [128, 8 * 128], BF16)
        blk = sb("blk", [65, 8 * 64], BF16)
        bsr = sb("bsr", [1, 8 * 128], BF16)
        bsr_s = sb("bsr_s", [1, 8 * 64], BF16)
        ones = sb("ones", [1, 128], BF16)
        brow = sb("brow", [1, 2 * 512], BF16)
        convw = sb("convw", [128, 8 * 3], F32)
        cst = sb("cst", [128, 8], F32)
        stat = sb("stat", [128, NCH * 16], F32)
        cs_all = sb("cs_all", [128, 8 * 34], F32)
        ci_s = sb("ci_s", [128, 8 * 96], F32)
        cwrow = sb("cwrow", [3, D], F32)

        ps = [es.enter_context(nc.psum_tensor("ps%d" % i, [128, 1024], F32)) for i in range(4)]

        def bank(b):
            return ps[b // 2][:, (b % 2) * 512:(b % 2) * 512 + 512]

        def bank_bf(b):
            return bank(b).bitcast(BF16)

        S = Sched(nc, es)
        B_bank = [Buf("bank%d" % i) for i in range(8)]
        B_ss = [Buf("ss%d" % i) for i in range(8)]

        MGf = MG[:, :].bitcast(F32)
        YAf0 = YA[:, :].bitcast(F32)
        XIN = [MGf[:, 0:1024], MGf[:, 1024:2048], YAf0[:, 2048:3072]]
        GV = [MGf[:, 2048:3072], MGf[:, 3072:4096], YAf0[:, 0:1024], YAf0[:, 1024:2048]]
        BC0 = MGf[:, 4096:5120]
        BC1 = MGf[:, 5120:6144]
        BC2 = MGf[:, 6144:7168]
        HTOK = [MG[:, 14336:15360], MG[:, 15360:16384], YA[:, 14336:15360]]
        WS_TOK = YAf0[:, 5120:6144]
        ST_TOK = YAf0[0:32, 6144:7168]
        JUNK = SCR[:, 0:512].bitcast(BF16)

        B_xin = [Buf("xin0"), Buf("xin1"), Buf("xin2")]
        B_gv = [Buf("gv%d" % i) for i in range(4)]
        B_htok = [Buf("htok0"), Buf("htok1"), Buf("htok2")]
        B_junk = Buf("junk")
        B_bc0, B_bc1, B_bc2 = Buf("bc0"), Buf("bc1"), Buf("bc2")
        B_stat = [Buf("stat%d" % c) for c in range(NCH)]
        B_hT = [Buf("hT%d" % c) for c in range(NCH)]
        B_va = [Buf("va%d" % c) for c in range(NCH)]
        B_ya = [Buf("ya%d" % h) for h in range(8)]
        B_yb = [Buf("yb%d" % j) for j in range(8)]
        B_mg = [[Buf("mg%d_%d" % (j, g)) for g in range(5)] for j in range(8)]
        B_mg_all = [b for row in B_mg for b in row]
        B_ring = [Buf("ring%d" % i) for i in range(NRING)]
        B_wpp = Buf("wpp")

        def st(c, i, r=128):
            return stat[0:r, c * 16 + i:c * 16 + i + 1]

        def cc(i, r=128):
            return cst[0:r, i:i + 1]
        C_NH, C_DEPS, C_LNEPS, C_INVD, C_M1 = 0, 1, 2, 3, 4

        B_id, B_mask, B_cst, B_ones, B_blk = Buf("ident"), Buf("mask"), Buf("cst"), Buf("ones"), Buf("blk")
        for i, val in enumerate([-0.5, D * EPS, LN_EPS, 1.0 / D, -1.0]):
            S.op("pool", MS(cst[:, i:i + 1], val), writes=[B_cst])

        def pool_setup_misc():
            S.op("pool", MS(identf[:], 0.0), writes=[B_id])
            S.op("pool", lambda e: e.affine_select(out=identf[:], in_=identf[:], pattern=[[-1, 128]],
                                                   compare_op=ALU.not_equal, fill=1.0, base=0, channel_multiplier=1),
                 writes=[B_id])
            S.op("pool", MS(maskf[:], 1.0), writes=[B_mask])
            S.op("pool", lambda e: e.affine_select(out=maskf[:], in_=maskf[:], pattern=[[1, 128]],
                                                   compare_op=ALU.is_ge, fill=0.0, base=0, channel_multiplier=-1),
                 writes=[B_mask])
            S.op("pool", MS(ones[:], 1.0), writes=[B_ones])
            S.op("pool", MS(blk[0:64, :], 0.0), writes=[B_blk])
        B_identb = Buf("identb")

        S.dma("sp", [DMA(XIN[0][0:128, :], x_d[0:128, :])], B_xin[0], writes=[B_xin[0]])
        S.dma("sp", [DMA(XIN[1][0:128, :], x_d[128:256, :])], B_xin[1], writes=[B_xin[1]])
        S.dma("sp", [DMA(BC0, norm_g_d.partition_broadcast(128))], B_bc0, writes=[B_bc0])
        B_ws = Buf("ws_tok")
        S.dma("sp", [DMA(WS_TOK.rearrange("t (h s) -> t h s", s=128), w_s_d.rearrange("h t s -> t h s"))],
              B_ws, writes=[B_ws])
        B_sttok = Buf("st_tok")
        S.dma("sp", [DMA(ST_TOK, st_d[:, :])], B_sttok, writes=[B_sttok])
        B_cw = Buf("cwrow")
        S.dma("sp", [DMA(cwrow[:], conv_w_d[:, :])], B_cw, writes=[B_cw])
        S.dma("sp", [DMA(BC1, ln_g_d.partition_broadcast(128))], B_bc1, writes=[B_bc1])
        S.dma("sp", [DMA(BC2, ln_b_d.partition_broadcast(128))], B_bc2, writes=[B_bc2])
        B_bsr = Buf("bsr")
        S.op("dve", TS(BC0, BC0, 32.0, None, ALU.mult), writes=[B_bc0])

        def wcols(src, c0):
            return src[:, c0:c0 + 256]

        blocks = []
        for i in range(4):
            blocks.append(wcols(w_in_d, 1024 + 256 * i))
        for hb in range(4):
            blocks.append(wcols(w_in_d, 256 * hb))
            blocks.append(wcols(w_in_d, 2048 + 256 * hb))
        for jb in range(4):
            blocks.append(wcols(w_in_d, 3072 + 256 * jb))
            blocks.append(wcols(w_in_d, 5120 + 256 * jb))
            blocks.append(wcols(w_in_d, 4096 + 256 * jb))
            blocks.append(wcols(w_in_d, 6144 + 256 * jb))
        for jb in range(4):
            blocks.append(wcols(w_in_d, 7168 + 256 * jb))
            blocks.append(wcols(w_in_d, 8192 + 256 * jb))
            blocks.append(wcols(w_a_d, 256 * jb))
            blocks.append(wcols(w_b_d, 256 * jb))
        for i in range(4):
            blocks.append(wcols(w_o_d, 256 * i))
        for i in range(4):
            blocks.append(wcols(w_pg_d, 256 * i))
        NBLK = len(blocks)
        state = {"next_load": 0}

        def issue_loads(upto):
            while state["next_load"] < min(upto, NBLK):
                i = state["next_load"]
                s = i % NRING
                src = blocks[i].rearrange("(k p) c -> p k c", p=128)
                dst = WR[:, s * 2048:(s + 1) * 2048].rearrange("p (k c) -> p k c", c=256)
                S.dma("pool", [DMA(dst, src)], B_ring[s], writes=[B_ring[s]])
                state["next_load"] += 1

        def wslot(i):
            return i % NRING

        def wk(i, k, c0=0, n=256):
            s = wslot(i)
            return WR[:, s * 2048 + k * 256 + c0: s * 2048 + k * 256 + c0 + n]

        issue_loads(4)
        pool_setup_misc()
        S.dma("pool", [DMA(bsr[:], b_s_d.rearrange("(o h) t -> o (h t)", o=1))], B_bsr, writes=[B_bsr])
        S.op("dve", CP(ident[:], identf[:]), reads=[B_id], writes=[B_identb])
        S.dma("pool", [DMA(WPP[:, :].rearrange("p (k c) -> p k c", c=D), w_pp_d.rearrange("(k p) c -> p k c", p=128))],
              B_wpp, writes=[B_wpp])

        B_wT = Buf("wT")
        S.group("pe", [TR(ps[0][:, h * 128:(h + 1) * 128], WS_TOK[:, h * 128:(h + 1) * 128], identf[:]) for h in range(8)],
                reads=[B_ws, B_id], writes=[B_bank[0], B_bank[1]])
        for h in range(8):
            S.op("dve", TT(wT[:, h * 128:(h + 1) * 128], ps[0][:, h * 128:(h + 1) * 128], maskf[:], ALU.mult),
                 reads=[B_bank[0], B_bank[1], B_mask], writes=[B_wT])
        wT3 = wT[:, :].rearrange("s (h t) -> s h t", t=128)
        blk3 = blk[:, :].rearrange("s (h t) -> s h t", t=64)
        B_bsrs = Buf("bsr_s")
        bsr3 = bsr[:, :].rearrange("o (h t) -> o h t", t=128)
        bsrs4 = bsr_s[:, :].rearrange("o (h q t) -> o h q t", q=16, t=4)
        S.group("pool", [CP(bsrs4[:, :, q, :], bsr3[:, :, 0:4]) for q in range(16)], reads=[B_bsr], writes=[B_bsrs])
        B_vaone = Buf("vaone")
        S.op("pool", MS(VA[64:65, 16 * D:17 * D], 1.0), writes=[B_vaone])
        B_convw = Buf("convw")
        B_cis = [Buf("ci_s%d" % j) for j in range(8)]
        S.group("pe", [TR(ps[1][:, j * 4:j * 4 + 3], cwrow[0:3, j * 128:(j + 1) * 128], identf[0:3, 0:3]) for j in range(8)],
                reads=[B_cw, B_id], writes=[B_bank[2]])
        S.op("dve", CP(convw[:, :].rearrange("p (j k) -> p j k", k=3),
                       ps[1][:, 0:32].rearrange("p (j k) -> p j k", k=4)[:, :, 0:3]),
             reads=[B_bank[2]], writes=[B_convw])
        S.group("pe", [TR(ps[1][:, 512 + j * 32:512 + j * 32 + 32], ST_TOK[:, j * 128:(j + 1) * 128], identf[0:32, 0:32])
                       for j in range(8)], reads=[B_sttok, B_id], writes=[B_bank[3]])
        ci_s4 = ci_s[:, :].rearrange("p (j q t) -> p j q t", q=16, t=6)
        for j in range(8):
            S.op("dve", CP(ci_s4[:, j, :, 0:2], ps[1][:, 512 + j * 32:512 + j * 32 + 32].rearrange("p (q r) -> p q r", r=2)),
                 reads=[B_bank[3]], writes=[B_cis[j]])

        out_bufs = []

        NV0 = 0
        HT3 = HT[:, :].rearrange("p (k t) -> p k t", t=NT)

        INVD = 1.0 / D

        def A_load(c):
            r, t0, sl = crow(c), c * 128, c % 3
            S.dma("sp", [DMA(XIN[sl][0:r, :], x_d[t0:t0 + r, :])], B_xin[sl], writes=[B_xin[sl]])

        def A0(c):
            r, sl, hs = crow(c), c % 3, c % 3
            S.op("act", ACT(HTOK[hs][0:r, :], XIN[sl][0:r, :], AF.Square, accum_out=st(c, 0, r)),
                 reads=[B_xin[sl]], writes=[B_htok[hs], B_stat[c]])
            S.op("pool", TT(st(c, 1, r), st(c, 0, r), cc(C_DEPS, r), ALU.add), reads=[B_cst], writes=[B_stat[c]])
            S.op("pool", TT(st(c, 2, r), st(c, 1, r), cc(C_NH, r), ALU.pow), reads=[B_cst], writes=[B_stat[c]])

        def A2(c):
            r, sl, hs = crow(c), c % 3, c % 3
            S.op("dve", STT(HTOK[hs][0:r, :], XIN[sl][0:r, :], st(c, 2, r), BC0[0:r, :], ALU.mult, ALU.mult),
                 reads=[B_xin[sl], B_stat[c], B_bc0], writes=[B_htok[hs]])

        def A3(c):
            r, sl, hs = crow(c), c % 2, c % 3
            pt = bank_bf(sl)
            S.group("pe", [TR(pt[:, k * 128:k * 128 + r], HTOK[hs][0:r, k * 128:(k + 1) * 128], ident[0:r, 0:r])
                           for k in range(8)], reads=[B_htok[hs], B_identb], writes=[B_bank[sl]])

        def A4(c):
            r, t0, sl = crow(c), c * 128, c % 2
            pt = bank_bf(sl)
            S.op("act", ACT(HT3[:, :, t0:t0 + r], pt.rearrange("p (k t) -> p k t", t=128)[:, :, 0:r], AF.Copy),
                 reads=[B_bank[sl]], writes=[B_hT[c]])

        def Bmm(c):
            r, t0, sl = crow(c), c * 128, c % 2
            pb = 2 + 2 * sl
            pv = ps[pb // 2]
            for q in range(4):
                fns = [MM(pv[0:r, q * 256:(q + 1) * 256], HT[:, k * NT + t0:k * NT + t0 + r], wk(NV0 + q, k),
                          k == 0, k == 7) for k in range(8)]
                S.group("pe", fns, reads=[B_hT[c], B_ring[wslot(NV0 + q)]], writes=[B_bank[pb + q // 2]])

        def Bact_g(c):
            r, sl, gs = crow(c), c % 2, c % 4
            pb = 2 + 2 * sl
            pv = ps[pb // 2]
            gv = GV[gs]
            S.op("act", ACT(gv[0:r, :], pv[0:r, :], AF.Gelu_apprx_tanh, accum_out=st(c, 3, r)),
                 reads=[B_bank[pb], B_bank[pb + 1]], writes=[B_gv[gs], B_stat[c]])

        def Bact_s(c):
            r, gs = crow(c), c % 4
            gv = GV[gs]
            S.op("act", ACT(JUNK[0:r, :], gv[0:r, :], AF.Square, accum_out=st(c, 4, r)),
                 reads=[B_gv[gs]], writes=[B_junk, B_stat[c]])

        def C1(c):
            r = crow(c)
            S.op("dve", TS(st(c, 5, r), st(c, 3, r), INVD, None, ALU.mult), writes=[B_stat[c]])
            S.op("dve", TS(st(c, 6, r), st(c, 5, r), st(c, 5, r), -LN_EPS, ALU.mult, ALU.add), writes=[B_stat[c]])
            S.op("dve", STT(st(c, 9, r), st(c, 4, r), INVD, st(c, 6, r), ALU.mult, ALU.subtract), writes=[B_stat[c]])

        def C2(c):
            r = crow(c)
            S.op("pool", TT(st(c, 10, r), st(c, 9, r), cc(C_NH, r), ALU.pow), reads=[B_cst], writes=[B_stat[c]])

        BC1b, BC2b = YA[:, 6144:7168], YA[:, 7168:8192]
        N16 = [YA[:, 8192:9216], YA[:, 9216:10240]]
        B_bc1b, B_bc2b = Buf("bc1b"), Buf("bc2b")
        B_n16 = [Buf("n16_0"), Buf("n16_1")]
        S.op("dve", CP(BC1b, BC1), reads=[B_bc1], writes=[B_bc1b])
        S.op("dve", CP(BC2b, BC2), reads=[B_bc2], writes=[B_bc2b])

        def C3(c):
            r, gs = crow(c), c % 4
            gv = GV[gs]
            va = VA[0:r, c * D:(c + 1) * D]
            S.op("dve", STT(st(c, 12, r), st(c, 5, r), -1.0, st(c, 10, r), ALU.mult, ALU.mult), writes=[B_stat[c]])
            if c < 15:
                n16 = N16[c % 2]
                S.op("dve", TS(n16[0:r, :], gv[0:r, :], st(c, 10, r), st(c, 12, r), ALU.mult, ALU.add),
                     reads=[B_stat[c], B_gv[gs]], writes=[B_n16[c % 2]])
                S.op("dve", TT(n16[0:r, :], n16[0:r, :], BC1b[0:r, :], ALU.mult), reads=[B_bc1b], writes=[B_n16[c % 2]])
                S.op("dve", TT(va, n16[0:r, :], BC2b[0:r, :], ALU.add), reads=[B_n16[c % 2], B_bc2b], writes=[B_va[c]])
            else:
                S.op("dve", TS(gv[0:r, :], gv[0:r, :], st(c, 10, r), st(c, 12, r), ALU.mult, ALU.add),
                     reads=[B_stat[c]], writes=[B_gv[gs]])
                S.op("dve", TT(gv[0:r, :], gv[0:r, :], BC1[0:r, :], ALU.mult), reads=[B_bc1], writes=[B_gv[gs]])
                S.op("dve", TT(gv[0:r, :], gv[0:r, :], BC2[0:r, :], ALU.add), reads=[B_bc2], writes=[B_gv[gs]])
                dst = vp_d if c == 15 else vs_d
                ob = Buf("vout%d" % c)
                S.dma("sp", [DMA(dst[:, :], gv[0:r, :])], ob, reads=[B_gv[gs]])
                out_bufs.append(ob)
                S.op("dve", CP(va, gv[0:r, :]), reads=[B_gv[gs]], writes=[B_va[c]])

        def okc(c):
            return 0 <= c < NCH
        ph1 = stop_after >= 1
        def emit_blk_dmas():
            S.dma("sp", [DMA(blk3[4 * q:4 * q + 4, :, 4 * q:4 * q + 4], wT3[0:4, :, 0:4]) for q in range(16)],
                  B_blk, reads=[B_wT], writes=[B_blk])
            B_blk2 = Buf("blk2")
            S.dma("sp", [DMA(blk[64:65, :], bsr_s[0:1, :])], B_blk2, reads=[B_bsrs], writes=[B_blk2])
            return B_blk2

        LAG = 0
        for i in range(NCH + 6 + LAG):
            if okc(i - 5 - LAG) and ph1:
                C1(i - 5 - LAG)
            if okc(i):
                A0(i)
            if okc(i - 5 - LAG) and ph1:
                C2(i - 5 - LAG)
            if okc(i - 1):
                A2(i - 1)
            if okc(i - 2):
                A3(i - 2)
            if okc(i - 3 - LAG) and ph1:
                Bmm(i - 3 - LAG)
            if okc(i - 4 - LAG) and ph1:
                Bact_g(i - 4 - LAG)
            if okc(i - 2):
                A4(i - 2)
            if okc(i - 4 - LAG) and ph1:
                Bact_s(i - 4 - LAG)
            if okc(i - 5 - LAG) and ph1:
                C3(i - 5 - LAG)
            if okc(i - 1):
                if okc(i + 1) and i + 1 >= 2:
                    A_load(i + 1)
            if i == 6:
                issue_loads(NRING)
            if i == 16:
                B_blk2 = emit_blk_dmas()

        ring_state = {"rr": 0, "ss": 0}

        def alloc_bank():
            b = ring_state["rr"] % 8
            ring_state["rr"] += 1
            return b

        def alloc_out(g):
            b = alloc_bank()
            return bank(b)[:, 0:GROUPS[g][1]], B_bank[b]

        def hT_bufs(g):
            return [B_hT[c] for (c, _, _, _) in group_chunks(g)]

        def proj(blk_i, c0, g, SRC, src_bufs):
            t0, n = GROUPS[g]
            o, ob = alloc_out(g)
            fns = [MM(o, wk(blk_i, k, c0, 128), SRC[:, k * NT + t0:k * NT + t0 + n], k == 0, k == 7) for k in range(8)]
            S.group("pe", fns, reads=list(src_bufs) + [B_ring[wslot(blk_i)]], writes=[ob])
            return o, ob

        def proj_hT(blk_i, c0, g):
            return proj(blk_i, c0, g, HT, hT_bufs(g))

        T1 = SCR[:, 0:NT]
        T2 = [SCR[:, NT:NT + 512], SCR[:, NT + 512:NT + 1024]]
        B_t1 = [Buf("t1_%d" % g) for g in range(5)]
        B_t2 = [Buf("t2_0"), Buf("t2_1")]
        for b in B_t1 + B_t2:
            b.inherit(B_junk)
        for b in B_ya:
            b.inherit(B_gv[2], B_gv[3], B_xin[2], B_bc1b, B_bc2b, B_ws, B_sttok, B_htok[2], *B_n16)

        if stop_after >= 2:
            NU0 = 4
            B_brow = [Buf("brow0"), Buf("brow1")]
            for h in range(8):
                hb, hc = h // 2, (h % 2) * 128
                bu, bg = NU0 + 2 * hb, NU0 + 2 * hb + 1
                if h % 2 == 0:
                    issue_loads(bg + 1 + NRING - 2)
                S.group("act", [ACT(brow[0:1, (h % 2) * 512 + 128 * r_:(h % 2) * 512 + 128 * (r_ + 1)],
                                    bsr[0:1, h * 128:(h + 1) * 128], AF.Copy) for r_ in range(4)],
                        reads=[B_bsr], writes=[B_brow[h % 2]])
                for g in range(5):
                    t0, n = GROUPS[g]
                    o, ob = proj_hT(bu, hc, g)
                    S.op("act", ACT(T1[:, t0:t0 + n], o, AF.Gelu_apprx_tanh), reads=[ob], writes=[B_t1[g]])
                for g in range(5):
                    t0, n = GROUPS[g]
                    sl = g % 2
                    o, ob = proj_hT(bg, hc, g)
                    S.op("act", ACT(T2[sl][:, 0:n], o, AF.Silu), reads=[ob], writes=[B_t2[sl]])
                    S.op("dve", TT(T2[sl][:, 0:n], T2[sl][:, 0:n], T1[:, t0:t0 + n], ALU.mult),
                         reads=[B_t1[g]], writes=[B_t2[sl]])
                    po, pob = alloc_out(g)
                    fns = []
                    rd = [B_brow[h % 2], B_ones, B_wT]
                    chs = group_chunks(g)
                    pch = [x for x in chs if x[0] < 16]
                    npp = sum(lb - la for (_, la, lb, _) in pch)
                    la0 = pch[0][1]
                    br = brow[0:1, (h % 2) * 512 + la0:(h % 2) * 512 + la0 + npp]
                    fns.append(MM(po[:, 0:npp], ones[0:1, :], br, True, False))
                    for (c, la, lb, off) in chs:
                        if c < 16:
                            fns.append(MM(po[:, off:off + lb - la], VA[:, c * D + h * 128:c * D + h * 128 + 128],
                                          wT[:, h * 128 + la:h * 128 + lb], False, c == pch[-1][0]))
                            rd.append(B_va[c])
                        else:
                            fns.append(MM(po[:, off:off + 64], VA[0:65, 16 * D + h * 128:16 * D + h * 128 + 128],
                                          blk[0:65, h * 64:(h + 1) * 64], True, True))
                            rd += [B_va[16], B_blk, B_blk2, B_vaone]
                    S.group("pe", fns, reads=rd, writes=[pob])
                    S.op("dve", TT(YA[:, h * NT + t0:h * NT + t0 + n], po, T2[sl][:, 0:n], ALU.mult),
                         reads=[pob, B_t2[sl]], writes=[B_ya[h]])

        YB = VA
        B_ci = [Buf("ci0"), Buf("ci1")]
        B_acc = [Buf("acc0"), Buf("acc1")]
        B_sg = [Buf("sg0"), Buf("sg1")]
        B_accs = Buf("accs")
        if stop_after >= 3:
            NC0 = 12
            for b in B_yb:
                b.inherit(*B_va)
            CI = [SCR[:, 0:514], SCR[:, 514:1028]]
            ACC = [SCR[:, 1028:1540], SCR[:, 1540:2052]]
            SG = [SCR[:, 2052:2564], SCR[:, 2564:3076]]
            ACCS = SCR[:, 3076:3140]
            for b in B_ci + B_acc + B_sg + [B_accs]:
                b.inherit(*(B_t1 + B_t2))
            B_csall = Buf("cs_all")
            cs3 = cs_all[:, :].rearrange("p (j m) -> p j m", m=34)
            for j in range(8):
                jb, jc = j // 2, (j % 2) * 128
                b0 = NC0 + 4 * jb
                if j % 2 == 0:
                    issue_loads(b0 + 4 + NRING - 4)
                w0, w1, w2 = (convw[:, j * 3 + k:j * 3 + k + 1] for k in range(3))
                for g in range(5):
                    t0, n = GROUPS[g]
                    sl = g % 2
                    pc, pcb = proj_hT(b0 + 0, jc, g)
                    ph, phb = proj_hT(b0 + 1, jc, g)
                    pbv, pbvb = proj_hT(b0 + 2, jc, g)
                    pg, pgb = proj_hT(b0 + 3, jc, g)
                    npr = min(t0 + n, SEQ) - t0
                    has_s = t0 + n > SEQ
                    ci = CI[sl]
                    acc = ACC[sl]
                    sg = SG[sl]
                    if g == 0:
                        S.op("pool", MS(ci[:, 0:2], 0.0), writes=[B_ci[sl]])
                    else:
                        pn = min(GROUPS[g - 1][0] + GROUPS[g - 1][1], SEQ) - GROUPS[g - 1][0]
                        S.op("pool", CP(ci[:, 0:2], CI[1 - sl][:, pn:pn + 2]), reads=[B_ci[1 - sl]], writes=[B_ci[sl]])
                    S.op("act", ACT(ci[:, 2:2 + npr], pc[:, 0:npr], AF.Copy), reads=[pcb], writes=[B_ci[sl]])
                    S.op("dve", TT(ci[:, 2:2 + npr], ci[:, 2:2 + npr], ph[:, 0:npr], ALU.mult), reads=[phb], writes=[B_ci[sl]])
                    S.op("act", ACT(acc[:, 0:npr], ci[:, 2:2 + npr], AF.Identity, scale=w2), reads=[B_ci[sl], B_convw],
                         writes=[B_acc[sl]])
                    S.op("dve", STT(acc[:, 0:npr], ci[:, 1:1 + npr], w1, acc[:, 0:npr], ALU.mult, ALU.add),
                         reads=[B_ci[sl], B_convw], writes=[B_acc[sl]])
                    S.op("dve", STT(acc[:, 0:npr], ci[:, 0:npr], w0, acc[:, 0:npr], ALU.mult, ALU.add),
                         reads=[B_ci[sl], B_convw], writes=[B_acc[sl]])
                    if g == 4:
                        S.op("pool", CP(cs3[:, j, 0:2], ci[:, npr:npr + 2]), reads=[B_ci[sl]], writes=[B_csall])
                    if has_s:
                        cj = ci_s4[:, j, :, :]
                        pcs_ = pc[:, npr:npr + 64].rearrange("p (q t) -> p q t", t=4)
                        phs_ = ph[:, npr:npr + 64].rearrange("p (q t) -> p q t", t=4)
                        S.op("act", ACT(cj[:, :, 2:6], pcs_, AF.Copy), reads=[pcb], writes=[B_cis[j]])
                        S.op("dve", TT(cj[:, :, 2:6], cj[:, :, 2:6], phs_, ALU.mult), reads=[phb], writes=[B_cis[j]])
                        acc3 = ACCS.rearrange("p (q t) -> p q t", t=4)
                        S.op("act", ACT(acc3, cj[:, :, 2:6], AF.Identity, scale=w2), reads=[B_cis[j], B_convw], writes=[B_accs])
                        S.op("dve", STT(acc3, cj[:, :, 1:5], w1, acc3, ALU.mult, ALU.add), reads=[B_cis[j], B_convw],
                             writes=[B_accs])
                        S.op("dve", STT(acc3, cj[:, :, 0:4], w0, acc3, ALU.mult, ALU.add), reads=[B_cis[j], B_convw],
                             writes=[B_accs])
                        S.op("pool", CP(cs3[:, j, 2:34].rearrange("p (q r) -> p q r", r=2), cj[:, :, 4:6]),
                             reads=[B_cis[j]], writes=[B_csall])
                    S.op("act", ACT(sg[:, 0:n], pg, AF.Silu), reads=[pgb], writes=[B_sg[sl]])
                    S.op("dve", TT(acc[:, 0:npr], acc[:, 0:npr], pbv[:, 0:npr], ALU.mult), reads=[pbvb], writes=[B_acc[sl]])
                    S.op("dve", TT(YB[:, j * NT + t0:j * NT + t0 + npr], acc[:, 0:npr], sg[:, 0:npr], ALU.mult),
                         reads=[B_acc[sl], B_sg[sl]], writes=[B_yb[j]])
                    if has_s:
                        S.op("dve", TT(ACCS, ACCS, pbv[:, npr:npr + 64], ALU.mult), reads=[pbvb], writes=[B_accs])
                        S.op("dve", TT(YB[:, j * NT + SEQ:j * NT + SEQ + 64], ACCS, sg[:, npr:npr + 64], ALU.mult),
                             reads=[B_accs, B_sg[sl]], writes=[B_yb[j]])
            if int(os.environ.get("K_SUB3", "9")) >= 3:
                csb = alloc_bank()
                while csb % 2 != 0:
                    csb = alloc_bank()
                alloc_bank()
                pcs = ps[csb // 2]
                S.group("pe", [TR(pcs[0:34, j * 128:(j + 1) * 128], cs3[:, j, :], identf[:]) for j in range(8)],
                        reads=[B_csall, B_id], writes=[B_bank[csb], B_bank[csb + 1]])
                CSO = SCR[0:34, 3200:4224]
                B_cso = Buf("cso")
                B_cso.inherit(*(B_t1 + B_t2))
                S.op("act", ACT(CSO, pcs[0:34, :], AF.Copy), reads=[B_bank[csb], B_bank[csb + 1]], writes=[B_cso])
                S.dma("sp", [DMA(csp_d[:, :], SCR[0:2, 3200:4224]), DMA(css_d[:, :], SCR[2:34, 3200:4224])],
                      B_cso, reads=[B_cso])
                out_bufs.append(B_cso)

        if stop_after >= 4:
            NM0 = 28
            for b in B_mg_all:
                b.inherit(*(B_xin + B_gv + B_htok + [B_bc0, B_bc1, B_bc2, B_ws, B_sttok]))
            TA = [SCR[:, 0:512], SCR[:, 512:1024]]
            TB = [SCR[:, 1024:1536], SCR[:, 1536:2048]]
            PP = [SCR[:, 2048:2560], SCR[:, 2560:3072]]
            B_ta = [Buf("ta0"), Buf("ta1")]
            B_tb = [Buf("tb0"), Buf("tb1")]
            B_pp = [Buf("pp0"), Buf("pp1")]
            for b in B_ta + B_tb + B_pp:
                b.inherit(*(B_ci + B_acc + B_sg + [B_accs] + B_t1 + B_t2))
            for j in range(8):
                jb, jc = j // 2, (j % 2) * 128
                b0 = NM0 + 4 * jb
                if j % 2 == 0:
                    issue_loads(b0 + 4 + NRING - 4)
                for g in range(5):
                    t0, n = GROUPS[g]
                    sl = g % 2
                    pma, pmab = proj_hT(b0 + 0, jc, g)
                    pmb, pmbb = proj_hT(b0 + 1, jc, g)
                    pa, pab = proj(b0 + 2, jc, g, YA, B_ya)
                    pbq, pbqb = proj(b0 + 3, jc, g, YB, B_yb)
                    S.op("act", ACT(TA[sl][:, 0:n], pma, AF.Tanh, scale=0.5), reads=[pmab], writes=[B_ta[sl]])
                    S.op("act", ACT(TB[sl][:, 0:n], pmb, AF.Tanh, scale=0.5), reads=[pmbb], writes=[B_tb[sl]])
                    S.op("dve", STT(PP[sl][:, 0:n], TA[sl][:, 0:n], 1.0, pa, ALU.add, ALU.mult),
                         reads=[B_ta[sl], pab], writes=[B_pp[sl]])
                    S.op("dve", STT(TB[sl][:, 0:n], TB[sl][:, 0:n], 1.0, pbq, ALU.add, ALU.mult),
                         reads=[pbqb], writes=[B_tb[sl]])
                    S.op("dve", TT(MG[:, j * NT + t0:j * NT + t0 + n], PP[sl][:, 0:n], TB[sl][:, 0:n], ALU.add),
                         reads=[B_pp[sl], B_tb[sl]], writes=[B_mg[j][g]])

        if stop_after >= 5:
            NO0, NG0 = 44, 48
            issue_loads(NBLK)
            dead = B_hT + B_ya + B_yb + B_va
            HTf = HT[:, :].bitcast(F32)
            YAf = YA[:, :].bitcast(F32)
            NXO = 5
            XO = [HTf[:, i * 1024:(i + 1) * 1024] for i in range(NXO)]
            TG = [HTf[:, 5120:6144], HTf[:, 6144:7168]]
            BCP = HTf[:, 7168:8192]
            YO = [YAf[:, 0:1024], YAf[:, 1024:2048]]
            BCF = YAf[:, 2048:3072]
            PIN = [YAf[:, 3072 + 256 * i:3072 + 256 * (i + 1)] for i in range(3)]
            o = 7680
            PB16 = [YA[:, o:o + 256], YA[:, o + 256:o + 512]]
            HN = [YA[:, o + 512:o + 1536], YA[:, o + 1536:o + 2560]]
            HNT = [YA[:, o + 2560:o + 3584], YA[:, o + 3584:o + 4608]]
            PT = [YA[:, o + 4608:o + 4864], YA[:, o + 4864:o + 5120]]
            JK = YA[:, o + 5120:o + 6144]
            B_xo = [Buf("xo%d" % i) for i in range(NXO)]
            B_tg = [Buf("tg0"), Buf("tg1")]
            B_yo = [Buf("yo0"), Buf("yo1")]
            B_bcp, B_bcf = Buf("bcp"), Buf("bcf")
            B_pin = [Buf("pin%d" % i) for i in range(3)]
            B_pb16 = [Buf("pb0"), Buf("pb1")]
            B_hn = [Buf("hn0"), Buf("hn1")]
            B_hnt = [Buf("hnt0"), Buf("hnt1")]
            B_pt = [Buf("pt0"), Buf("pt1")]
            B_jk = Buf("jk")
            B_st5 = [Buf("st5_%d" % c) for c in range(NCH)]
            for b in B_xo + B_tg + B_yo + [B_bcp, B_bcf] + B_pin + B_pb16 + B_hn + B_hnt + B_pt + [B_jk]:
                b.inherit(*dead)
            for b in B_st5:
                b.inherit(*B_stat)
            S.dma("sp", [DMA(BCP, pe_g_d.partition_broadcast(128))], B_bcp, writes=[B_bcp])
            S.dma("sp", [DMA(BCF, fin_g_d.partition_broadcast(128))], B_bcf, writes=[B_bcf])
            S.op("dve", TS(BCP, BCP, 16.0, None, ALU.mult), writes=[B_bcp])
            S.op("dve", TS(BCF, BCF, 32.0, None, ALU.mult), writes=[B_bcf])

            def load5(c):
                r, t0 = crow(c), c * 128
                S.dma("sp", [DMA(XO[c % NXO][0:r, :], x_d[t0:t0 + r, :])], B_xo[c % NXO], writes=[B_xo[c % NXO]])
                S.dma("sp", [DMA(PIN[c % 3][0:r, :], p_d[t0:t0 + r, :])], B_pin[c % 3], writes=[B_pin[c % 3]])

            pA, pG, pQ = ps[0], ps[1], ps[2]

            def s_wo(c):
                r, t0 = crow(c), c * 128
                fns = []
                for q in range(4):
                    for k in range(8):
                        fns.append(MM(pA[0:r, q * 256:(q + 1) * 256], MG[:, k * NT + t0:k * NT + t0 + r], wk(NO0 + q, k),
                                      k == 0, k == 7))
                gs_ = [g for g in range(5) if any(cc_ == c for (cc_, _, _, _) in group_chunks(g))]
                S.group("pe", fns, reads=[B_mg[j][g] for j in range(8) for g in gs_] + [B_ring[wslot(NO0 + q)] for q in range(4)],
                        writes=[B_bank[0], B_bank[1]])

            def s_sqf(c):
                r = crow(c)
                S.op("act", ACT(JK[0:r, :], XO[c % NXO][0:r, :], AF.Square, accum_out=st(c, 10, r)),
                     reads=[B_xo[c % NXO]], writes=[B_jk, B_st5[c]])

            def s_pb16(c):
                r = crow(c)
                S.op("pool", CP(PB16[c % 2][0:r, :], PIN[c % 3][0:r, :]), reads=[B_pin[c % 3]], writes=[B_pb16[c % 2]])

            def s_r3(c):
                r = crow(c)
                S.op("pool", TT(st(c, 11, r), st(c, 10, r), cc(C_DEPS, r), ALU.add), reads=[B_cst], writes=[B_st5[c]])
                S.op("pool", TT(st(c, 12, r), st(c, 11, r), cc(C_NH, r), ALU.pow), reads=[B_cst], writes=[B_st5[c]])

            def s_xo(c):
                r = crow(c)
                xo = XO[c % NXO]
                S.op("dve", STT(xo[0:r, :], pA[0:r, :], 0.5, xo[0:r, :], ALU.mult, ALU.add),
                     reads=[B_bank[0], B_bank[1]], writes=[B_xo[c % NXO]])

            def s_sqa(c):
                r = crow(c)
                S.op("act", ACT(JK[0:r, :], XO[c % NXO][0:r, :], AF.Square, accum_out=st(c, 13, r)),
                     reads=[B_xo[c % NXO]], writes=[B_jk, B_st5[c]])

            def s_hn(c):
                r = crow(c)
                S.op("pool", TT(HN[c % 2][0:r, :], XO[c % NXO][0:r, :], BCP[0:r, :], ALU.mult),
                     reads=[B_xo[c % NXO], B_bcp], writes=[B_hn[c % 2]])

            def s_r2(c):
                r = crow(c)
                S.op("pool", TT(st(c, 14, r), st(c, 13, r), cc(C_DEPS, r), ALU.add), reads=[B_cst], writes=[B_st5[c]])
                S.op("pool", TT(st(c, 15, r), st(c, 14, r), cc(C_NH, r), ALU.pow), reads=[B_cst], writes=[B_st5[c]])

            def s_y(c):
                r, t0 = crow(c), c * 128
                yo = YO[c % 2]
                S.op("dve", STT(yo[0:r, :], XO[c % NXO][0:r, :], st(c, 12, r), BCF[0:r, :], ALU.mult, ALU.mult),
                     reads=[B_xo[c % NXO], B_st5[c], B_bcf], writes=[B_yo[c % 2]])
                S.dma("sp", [DMA(y_d[t0:t0 + r, :], yo[0:r, :])], B_yo[c % 2], reads=[B_yo[c % 2]])

            def s_wpg(c):
                r = crow(c)
                sl = c % 2
                fns = []
                for q in range(4):
                    for k in range(8):
                        fns.append(MM(pG[0:r, q * 256:(q + 1) * 256], HNT[sl][:, k * 128:k * 128 + r], wk(NG0 + q, k),
                                      k == 0, k == 7))
                S.group("pe", fns, reads=[B_hnt[sl]] + [B_ring[wslot(NG0 + q)] for q in range(4)],
                        writes=[B_bank[2], B_bank[3]])
                fns = []
                for q in range(4):
                    for k in range(2):
                        fns.append(MM(pQ[0:r, q * 256:(q + 1) * 256], PT[sl][:, k * 128:k * 128 + r],
                                      WPP[:, k * D + q * 256:k * D + q * 256 + 256], k == 0, k == 1))
                S.group("pe", fns, reads=[B_pt[sl], B_wpp], writes=[B_bank[4], B_bank[5]])

            def s_tanh(c):
                r = crow(c)
                S.op("act", ACT(TG[c % 2][0:r, :], pG[0:r, :], AF.Tanh, scale=st(c, 15, r)),
                     reads=[B_bank[2], B_bank[3], B_st5[c]], writes=[B_tg[c % 2]])

            def s_qxf(c):
                r = crow(c)
                tg, xo = TG[c % 2], XO[c % NXO]
                S.op("dve", STT(tg[0:r, :], tg[0:r, :], 1.0, pQ[0:r, :], ALU.add, ALU.mult),
                     reads=[B_bank[4], B_bank[5]], writes=[B_tg[c % 2]])
                S.op("dve", STT(xo[0:r, :], tg[0:r, :], 0.5, xo[0:r, :], ALU.mult, ALU.add),
                     reads=[B_tg[c % 2]], writes=[B_xo[c % NXO]])

            def s_tr(c):
                r = crow(c)
                sl = c % 2
                pt, pt2 = bank_bf(6), bank_bf(7)
                S.group("pe", [TR(pt[:, k * 128:k * 128 + r], HN[sl][0:r, k * 128:(k + 1) * 128], ident[0:r, 0:r])
                               for k in range(8)], reads=[B_hn[sl], B_identb], writes=[B_bank[6]])
                S.group("pe", [TR(pt2[:, k * 128:k * 128 + r], PB16[sl][0:r, k * 128:(k + 1) * 128], ident[0:r, 0:r])
                               for k in range(2)], reads=[B_pb16[sl], B_identb], writes=[B_bank[7]])

            def s_hnt(c):
                r = crow(c)
                sl = c % 2
                pt = bank_bf(6)
                S.op("act", ACT(HNT[sl].rearrange("p (k t) -> p k t", t=128)[:, :, 0:r],
                                pt.rearrange("p (k t) -> p k t", t=128)[:, :, 0:r], AF.Copy),
                     reads=[B_bank[6]], writes=[B_hnt[sl]])

            def s_ptc(c):
                r = crow(c)
                sl = c % 2
                pt2 = bank_bf(7)
                S.op("dve", CP(PT[sl].rearrange("p (k t) -> p k t", t=128)[:, :, 0:r],
                               pt2[:, 0:256].rearrange("p (k t) -> p k t", t=128)[:, :, 0:r]),
                     reads=[B_bank[7]], writes=[B_pt[sl]])

            load5(0)
            load5(1)

            def ok(c):
                return 0 <= c < NCH
            for i in range(NCH + 2):
                if ok(i):
                    s_wo(i)
                if ok(i - 2):
                    s_sqf(i - 2)
                if ok(i):
                    s_pb16(i)
                if ok(i - 2):
                    s_r3(i - 2)
                if ok(i):
                    s_xo(i)
                    s_sqa(i)
                    s_hn(i)
                    s_r2(i)
                if ok(i - 2):
                    s_y(i - 2)
                if ok(i - 1):
                    s_wpg(i - 1)
                    s_tanh(i - 1)
                    s_qxf(i - 1)
                if ok(i):
                    s_tr(i)
                    s_hnt(i)
                    s_ptc(i)
                if ok(i + 2):
                    load5(i + 2)
            out_bufs += B_yo

        def tap(name, src, bufs):
            if name in dbg_d:
                bd = Buf("dbg_" + name)
                S.dma("sp", [DMA(dbg_d[name][:, :], src)], bd, reads=bufs)
                out_bufs.append(bd)
        tap("d_hT", HT[:, :], B_hT)
        tap("d_va", VA[:, :], B_va + B_yb)
        tap("d_ya", YA[:, :], B_ya)
        tap("d_mg", MG[:, :], B_mg_all)

        S.wait_all("sp", out_bufs)
        with nc.Block() as block:
            S.emit(block)
    return nc


def make_in_maps(inputs, n_cores=N_CORES):
    f = lambda a: np.ascontiguousarray(np.asarray(a, dtype=np.float32))
    xp, xs = f(inputs["x_prompt"]), f(inputs["x_sample"])
    pp, psm = f(inputs["p_prompt"])[0], f(inputs["p_sample"])[0]
    stc = f(inputs["state_conv"])[0]
    shared = {
        "w_in": f(inputs["w_in"])[0], "w_a": f(inputs["w_a_out"])[0], "w_b": f(inputs["w_b_out"])[0],
        "w_o": f(inputs["w_o"])[0], "w_pg": f(inputs["w_pe_gate"])[0], "w_pp": f(inputs["w_pe_proj"])[0],
        "norm_g": f(inputs["norm_g"])[0], "ln_g": f(inputs["ln_v_g"])[0], "ln_b": f(inputs["ln_v_b"])[0],
        "pe_g": f(inputs["pe_norm_g"])[0], "fin_g": f(inputs["final_norm_g"]),
        "w_s": f(inputs["w_s"])[0], "b_s": f(inputs["b_s"])[0], "conv_w": f(inputs["conv_w"])[0],
    }
    maps = []
    for c in range(n_cores):
        m = dict(shared)
        m["x"] = np.ascontiguousarray(np.concatenate([xp[c], xs[16 * c:16 * c + 16].reshape(NS, D)], axis=0))
        m["p"] = np.ascontiguousarray(np.concatenate([pp[c], psm[16 * c:16 * c + 16].reshape(NS, P_DIM)], axis=0))
        m["st"] = np.ascontiguousarray(stc[16 * c:16 * c + 16].reshape(32, D))
        maps.append(m)
    return maps


_NC_CACHE = {}


def kernel(**inputs):
    if "nc" not in _NC_CACHE:
        _NC_CACHE["nc"] = build_program()
    nc = _NC_CACHE["nc"]
    maps = make_in_maps(inputs)
    res = run_bass_kernel_spmd(nc, maps, core_ids=list(range(N_CORES)))
    R = res.results
    y_prompt = np.stack([R[c]["y"][:SEQ] for c in range(N_CORES)], axis=0)
    y_sample = np.concatenate([R[c]["y"][SEQ:].reshape(16, 4, D) for c in range(N_CORES)], axis=0)
    cs_p = np.stack([R[c]["csp"] for c in range(N_CORES)], axis=0)[None]
    cs_s = np.concatenate([R[c]["css"].reshape(16, 2, D) for c in range(N_CORES)], axis=0)[None]
    v_p = np.stack([R[c]["vp"] for c in range(N_CORES)], axis=0)[None]
    v_s = np.concatenate([R[c]["vs"].reshape(16, 4, D) for c in range(N_CORES)], axis=0)[None]
    return (y_prompt.astype(np.float32), y_sample.astype(np.float32), cs_p.astype(np.float32),
            cs_s.astype(np.float32), v_p.astype(np.float32), v_s.astype(np.float32))
```

```python
import numpy as np
import concourse.bass as bass
import concourse.mybir as mybir
from concourse.bass_utils import run_bass_kernel_spmd
from contextlib import ExitStack

F32 = mybir.dt.float32
BF16 = mybir.dt.bfloat16
AF = mybir.ActivationFunctionType
ALU = mybir.AluOpType

D = 1024
SEQ = 2048
NS = 64
NT = SEQ + NS
NCH = 17
P_DIM = 256
EPS = 1e-6
LN_EPS = 1e-5
GROUPS = [(0, 512), (512, 512), (1024, 512), (1536, 288), (1824, 288)]
NRING = 8
N_CORES = 8


def crow(c):
    return 128 if c < 16 else 64


def group_chunks(g):
    t0, n = GROUPS[g]
    out = []
    for c in range(NCH):
        a, b = max(t0, c * 128), min(t0 + n, c * 128 + crow(c))
        if a < b:
            out.append((c, a - c * 128, b - c * 128, a - t0))
    return out


class Buf:
    __slots__ = ("name", "w", "r", "dsem", "dcnt")

    def __init__(self, name):
        self.name = name
        self.w = {}
        self.r = {}
        self.dsem = None
        self.dcnt = 0

    def inherit(self, *others):
        for o in others:
            for d in (o.w, o.r):
                for k, (s, v) in d.items():
                    if k not in self.r or self.r[k][1] < v:
                        self.r[k] = (s, v)


def _merge(d, tok):
    k = id(tok[0])
    if k not in d or d[k][1] < tok[1]:
        d[k] = tok


class Sched:
    CE = ("pe", "act", "dve", "pool")

    def __init__(self, nc, stack):
        self.nc = nc
        self.stack = stack
        self.q = {e: [] for e in ("pe", "act", "dve", "pool", "sp")}
        self.sem = {e: stack.enter_context(nc.semaphore("s_" + e)) for e in self.CE}
        self.cnt = {e: 0 for e in self.CE}
        self.waited = {e: {} for e in self.q}
        self.nsem = 4

    def new_sem(self, name):
        self.nsem += 1
        return self.stack.enter_context(self.nc.semaphore(name))

    def _waits(self, eng, reads, writes):
        need = {}
        for b in reads:
            for tok in b.w.values():
                _merge(need, tok)
        for b in writes:
            for tok in b.w.values():
                _merge(need, tok)
            for tok in b.r.values():
                _merge(need, tok)
        wd = self.waited[eng]
        for k, (s, v) in need.items():
            if wd.get(k, 0) >= v:
                continue
            wd[k] = v
            self.q[eng].append(("wait", s, v))

    def _commit(self, tok, reads, writes):
        for b in writes:
            b.w = {id(tok[0]): tok}
            b.r = {}
        for b in reads:
            _merge(b.r, tok)

    def op(self, eng, fn, reads=(), writes=()):
        return self.group(eng, [fn], reads, writes)

    def group(self, eng, fns, reads=(), writes=()):
        self._waits(eng, reads, writes)
        self.cnt[eng] += 1
        tok = (self.sem[eng], self.cnt[eng])
        for f in fns[:-1]:
            self.q[eng].append(("op", f, None))
        self.q[eng].append(("op", fns[-1], (self.sem[eng], 1)))
        self._commit(tok, reads, writes)
        return tok

    def dma(self, eng, fns, owner, reads=(), writes=()):
        if owner.dsem is None:
            owner.dsem = self.new_sem("d_" + owner.name)
        self._waits(eng, reads, writes)
        for f in fns:
            owner.dcnt += 16
            self.q[eng].append(("op", f, (owner.dsem, 16)))
        tok = (owner.dsem, owner.dcnt)
        self._commit(tok, reads, writes)
        _merge(owner.r, tok)
        return tok

    def wait_all(self, eng, bufs):
        self._waits(eng, (), bufs)

    def emit(self, block):
        nc = self.nc
        table = {"pe": block.tensor, "act": block.scalar, "dve": block.vector, "pool": block.gpsimd, "sp": block.sync}
        for name, deco in table.items():
            items = self.q[name]

            def body(e, items=items):
                for it in items:
                    if it[0] == "wait":
                        e.wait_ge(it[1], it[2])
                    else:
                        ins = it[1](e)
                        if it[2] is not None:
                            ins.then_inc(it[2][0], it[2][1])
            deco(body)


def MM(out, lhsT, rhs, start, stop):
    return lambda e: e.matmul(out, lhsT=lhsT, rhs=rhs, start=start, stop=stop)


def TR(out, in_, idn):
    return lambda e: e.transpose(out, in_, idn)


def ACT(out, in_, func, **kw):
    return lambda e: e.activation(out=out, in_=in_, func=func, **kw)


def TT(out, in0, in1, op):
    return lambda e: e.tensor_tensor(out=out, in0=in0, in1=in1, op=op)


def STT(out, in0, scalar, in1, op0, op1):
    return lambda e: e.scalar_tensor_tensor(out=out, in0=in0, scalar=scalar, in1=in1, op0=op0, op1=op1)


def TS(out, in0, s1, s2, op0, op1=None):
    if op1 is None:
        return lambda e: e.tensor_scalar(out=out, in0=in0, scalar1=s1, scalar2=None, op0=op0)
    return lambda e: e.tensor_scalar(out=out, in0=in0, scalar1=s1, scalar2=s2, op0=op0, op1=op1)


def CP(out, in_):
    return lambda e: e.tensor_copy(out=out, in_=in_)


def MS(ap, val):
    return lambda e: e.memset(ap, val)


def DMA(out, in_):
    return lambda e: e.dma_start(out=out, in_=in_)

def build_program(stop_after=99, dbg=()):
    nc = bass.Bass("TRN2", target_bir_lowering=False)

    def din(name, shape):
        return nc.dram_tensor(name, list(shape), F32, kind="ExternalInput").ap()

    def dout(name, shape):
        return nc.dram_tensor(name, list(shape), F32, kind="ExternalOutput").ap()

    x_d = din("x", [NT, D])
    p_d = din("p", [NT, P_DIM])
    st_d = din("st", [32, D])
    w_in_d = din("w_in", [D, 9 * D])
    w_a_d = din("w_a", [D, D])
    w_b_d = din("w_b", [D, D])
    w_o_d = din("w_o", [D, D])
    w_pg_d = din("w_pg", [D, D])
    w_pp_d = din("w_pp", [P_DIM, D])
    norm_g_d = din("norm_g", [D])
    ln_g_d = din("ln_g", [D])
    ln_b_d = din("ln_b", [D])
    pe_g_d = din("pe_g", [D])
    fin_g_d = din("fin_g", [D])
    w_s_d = din("w_s", [8, 128, 128])
    b_s_d = din("b_s", [8, 128])
    conv_w_d = din("conv_w", [3, D])
    y_d = dout("y", [NT, D])
    csp_d = dout("csp", [2, D])
    css_d = dout("css", [32, D])
    vp_d = dout("vp", [128, D])
    vs_d = dout("vs", [64, D])
    dbg_d = {}
    for name, shape, dt in dbg:
        dbg_d[name] = nc.dram_tensor(name, list(shape), dt, kind="ExternalOutput").ap()

    with ExitStack() as es:
        def sb(name, shape, dt):
            return es.enter_context(nc.sbuf_tensor(name, list(shape), dt))

        HT = sb("HT", [128, 8 * NT], BF16)
        VA = sb("VA", [128, 17 * D], BF16)
        YA = sb("YA", [128, 8 * NT], BF16)
        MG = sb("MG", [128, 8 * NT], BF16)
        WR = sb("WR", [128, NRING * 2048], BF16)
        WPP = sb("WPP", [128, 2 * D], BF16)
        SCR = sb("SCR", [128, 4224], F32)
        ident = sb("ident", [128, 128], BF16)
        identf = sb("identf", [128, 128], F32)
        maskf = sb("maskf", [128, 128], F32)
        wT = sb("wT", [128, 8 * 128], BF16)
        blk = sb("blk", [65, 8 * 64], BF16)
        bsr = sb("bsr", [1, 8 * 128], BF16)
        bsr_s = sb("bsr_s", [1, 8 * 64], BF16)
        ones = sb("ones", [1, 128], BF16)
        brow = sb("brow", [1, 2 * 512], BF16)
        convw = sb("convw", [128, 8 * 3], F32)
        cst = sb("cst", [128, 8], F32)
        stat = sb("stat", [128, NCH * 16], F32)
        cs_all = sb("cs_all", [128, 8 * 34], F32)
        ci_s = sb("ci_s", [128, 8 * 96], F32)
        cwrow = sb("cwrow", [3, D], F32)

        ps = [es.enter_context(nc.psum_tensor("ps%d" % i, [128, 1024], F32)) for i in range(4)]

        def bank(b):
            return ps[b // 2][:, (b % 2) * 512:(b % 2) * 512 + 512]

        def bank_bf(b):
            return bank(b).bitcast(BF16)

        S = Sched(nc, es)
        B_bank = [Buf("bank%d" % i) for i in range(8)]
        B_ss = [Buf("ss%d" % i) for i in range(8)]

        MGf = MG[:, :].bitcast(F32)
        YAf0 = YA[:, :].bitcast(F32)
        XIN = [MGf[:, 0:1024], MGf[:, 1024:2048], YAf0[:, 2048:3072]]
        GV = [MGf[:, 2048:3072], MGf[:, 3072:4096], YAf0[:, 0:1024], YAf0[:, 1024:2048]]
        BC0 = MGf[:, 4096:5120]
        BC1 = MGf[:, 5120:6144]
        BC2 = MGf[:, 6144:7168]
        HTOK = [MG[:, 14336:15360], MG[:, 15360:16384], YA[:, 14336:15360]]
        WS_TOK = YAf0[:, 5120:6144]
        ST_TOK = YAf0[0:32, 6144:7168]
        JUNK = SCR[:, 0:512].bitcast(BF16)

        B_xin = [Buf("xin0"), Buf("xin1"), Buf("xin2")]
        B_gv = [Buf("gv%d" % i) for i in range(4)]
        B_htok = [Buf("htok0"), Buf("htok1"), Buf("htok2")]
        B_junk = Buf("junk")
        B_bc0, B_bc1, B_bc2 = Buf("bc0"), Buf("bc1"), Buf("bc2")
        B_stat = [Buf("stat%d" % c) for c in range(NCH)]
        B_hT = [Buf("hT%d" % c) for c in range(NCH)]
        B_va = [Buf("va%d" % c) for c in range(NCH)]
        B_ya = [Buf("ya%d" % h) for h in range(8)]
        B_yb = [Buf("yb%d" % j) for j in range(8)]
        B_mg = [[Buf("mg%d_%d" % (j, g)) for g in range(5)] for j in range(8)]
        B_mg_all = [b for row in B_mg for b in row]
        B_ring = [Buf("ring%d" % i) for i in range(NRING)]
        B_wpp = Buf("wpp")

        def st(c, i, r=128):
            return stat[0:r, c * 16 + i:c * 16 + i + 1]

        def cc(i, r=128):
            return cst[0:r, i:i + 1]
        C_NH, C_DEPS, C_LNEPS, C_INVD, C_M1 = 0, 1, 2, 3, 4

        B_id = Buf("ident")
        S.op("pool", MS(identf[:], 0.0), writes=[B_id])
        S.op("pool", lambda e: e.affine_select(out=identf[:], in_=identf[:], pattern=[[-1, 128]],
                                               compare_op=ALU.not_equal, fill=1.0, base=0, channel_multiplier=1),
             writes=[B_id])
        B_mask = Buf("mask")
        S.op("pool", MS(maskf[:], 1.0), writes=[B_mask])
        S.op("pool", lambda e: e.affine_select(out=maskf[:], in_=maskf[:], pattern=[[1, 128]],
                                               compare_op=ALU.is_ge, fill=0.0, base=0, channel_multiplier=-1),
             writes=[B_mask])
        B_cst = Buf("cst")
        for i, val in enumerate([-0.5, D * EPS, LN_EPS, 1.0 / D, -1.0]):
            S.op("pool", MS(cst[:, i:i + 1], val), writes=[B_cst])
        B_ones = Buf("ones")
        S.op("pool", MS(ones[:], 1.0), writes=[B_ones])
        B_blk = Buf("blk")
        S.op("pool", MS(blk[0:64, :], 0.0), writes=[B_blk])
        B_identb = Buf("identb")
        S.op("dve", CP(ident[:], identf[:]), reads=[B_id], writes=[B_identb])

        S.dma("sp", [DMA(XIN[0][0:128, :], x_d[0:128, :])], B_xin[0], writes=[B_xin[0]])
        S.dma("sp", [DMA(XIN[1][0:128, :], x_d[128:256, :])], B_xin[1], writes=[B_xin[1]])
        S.dma("sp", [DMA(BC0, norm_g_d.partition_broadcast(128))], B_bc0, writes=[B_bc0])
        B_ws = Buf("ws_tok")
        S.dma("sp", [DMA(WS_TOK.rearrange("t (h s) -> t h s", s=128), w_s_d.rearrange("h t s -> t h s"))],
              B_ws, writes=[B_ws])
        B_sttok = Buf("st_tok")
        S.dma("sp", [DMA(ST_TOK, st_d[:, :])], B_sttok, writes=[B_sttok])
        B_cw = Buf("cwrow")
        S.dma("sp", [DMA(cwrow[:], conv_w_d[:, :])], B_cw, writes=[B_cw])
        S.dma("sp", [DMA(BC1, ln_g_d.partition_broadcast(128))], B_bc1, writes=[B_bc1])
        S.dma("sp", [DMA(BC2, ln_b_d.partition_broadcast(128))], B_bc2, writes=[B_bc2])
        B_bsr = Buf("bsr")
        S.dma("pool", [DMA(bsr[:], b_s_d.rearrange("(o h) t -> o (h t)", o=1))], B_bsr, writes=[B_bsr])
        S.op("dve", TS(BC0, BC0, 32.0, None, ALU.mult), writes=[B_bc0])

        def wcols(src, c0):
            return src[:, c0:c0 + 256]

        blocks = []
        for i in range(4):
            blocks.append(wcols(w_in_d, 1024 + 256 * i))
        for hb in range(4):
            blocks.append(wcols(w_in_d, 256 * hb))
            blocks.append(wcols(w_in_d, 2048 + 256 * hb))
        for jb in range(4):
            blocks.append(wcols(w_in_d, 3072 + 256 * jb))
            blocks.append(wcols(w_in_d, 5120 + 256 * jb))
            blocks.append(wcols(w_in_d, 4096 + 256 * jb))
            blocks.append(wcols(w_in_d, 6144 + 256 * jb))
        for jb in range(4):
            blocks.append(wcols(w_in_d, 7168 + 256 * jb))
            blocks.append(wcols(w_in_d, 8192 + 256 * jb))
            blocks.append(wcols(w_a_d, 256 * jb))
            blocks.append(wcols(w_b_d, 256 * jb))
        for i in range(4):
            blocks.append(wcols(w_o_d, 256 * i))
        for i in range(4):
            blocks.append(wcols(w_pg_d, 256 * i))
        NBLK = len(blocks)
        state = {"next_load": 0}

        def issue_loads(upto):
            while state["next_load"] < min(upto, NBLK):
                i = state["next_load"]
                s = i % NRING
                src = blocks[i].rearrange("(k p) c -> p k c", p=128)
                dst = WR[:, s * 2048:(s + 1) * 2048].rearrange("p (k c) -> p k c", c=256)
                S.dma("pool", [DMA(dst, src)], B_ring[s], writes=[B_ring[s]])
                state["next_load"] += 1

        def wslot(i):
            return i % NRING

        def wk(i, k, c0=0, n=256):
            s = wslot(i)
            return WR[:, s * 2048 + k * 256 + c0: s * 2048 + k * 256 + c0 + n]

        S._waits("pool", [B_xin[0], B_xin[1], B_bc0], ())
        issue_loads(4)
        S.dma("pool", [DMA(WPP[:, :].rearrange("p (k c) -> p k c", c=D), w_pp_d.rearrange("(k p) c -> p k c", p=128))],
              B_wpp, writes=[B_wpp])

        B_wT = Buf("wT")
        S.group("pe", [TR(ps[0][:, h * 128:(h + 1) * 128], WS_TOK[:, h * 128:(h + 1) * 128], identf[:]) for h in range(8)],
                reads=[B_ws, B_id], writes=[B_bank[0], B_bank[1]])
        for h in range(8):
            S.op("dve", TT(wT[:, h * 128:(h + 1) * 128], ps[0][:, h * 128:(h + 1) * 128], maskf[:], ALU.mult),
                 reads=[B_bank[0], B_bank[1], B_mask], writes=[B_wT])
        wT3 = wT[:, :].rearrange("s (h t) -> s h t", t=128)
        blk3 = blk[:, :].rearrange("s (h t) -> s h t", t=64)
        B_bsrs = Buf("bsr_s")
        bsr3 = bsr[:, :].rearrange("o (h t) -> o h t", t=128)
        bsrs4 = bsr_s[:, :].rearrange("o (h q t) -> o h q t", q=16, t=4)
        S.group("pool", [CP(bsrs4[:, :, q, :], bsr3[:, :, 0:4]) for q in range(16)], reads=[B_bsr], writes=[B_bsrs])
        B_vaone = Buf("vaone")
        S.op("pool", MS(VA[64:65, 16 * D:17 * D], 1.0), writes=[B_vaone])
        B_convw = Buf("convw")
        B_cis = [Buf("ci_s%d" % j) for j in range(8)]
        S.group("pe", [TR(ps[1][:, j * 4:j * 4 + 3], cwrow[0:3, j * 128:(j + 1) * 128], identf[0:3, 0:3]) for j in range(8)],
                reads=[B_cw, B_id], writes=[B_bank[2]])
        S.op("dve", CP(convw[:, :].rearrange("p (j k) -> p j k", k=3),
                       ps[1][:, 0:32].rearrange("p (j k) -> p j k", k=4)[:, :, 0:3]),
             reads=[B_bank[2]], writes=[B_convw])
        S.group("pe", [TR(ps[1][:, 512 + j * 32:512 + j * 32 + 32], ST_TOK[:, j * 128:(j + 1) * 128], identf[0:32, 0:32])
                       for j in range(8)], reads=[B_sttok, B_id], writes=[B_bank[3]])
        ci_s4 = ci_s[:, :].rearrange("p (j q t) -> p j q t", q=16, t=6)
        for j in range(8):
            S.op("dve", CP(ci_s4[:, j, :, 0:2], ps[1][:, 512 + j * 32:512 + j * 32 + 32].rearrange("p (q r) -> p q r", r=2)),
                 reads=[B_bank[3]], writes=[B_cis[j]])

        out_bufs = []

        NV0 = 0
        HT3 = HT[:, :].rearrange("p (k t) -> p k t", t=NT)

        INVD = 1.0 / D

        def A_load(c):
            r, t0, sl = crow(c), c * 128, c % 3
            S.dma("sp", [DMA(XIN[sl][0:r, :], x_d[t0:t0 + r, :])], B_xin[sl], writes=[B_xin[sl]])

        def A0(c):
            r, sl, hs = crow(c), c % 3, c % 3
            S.op("act", ACT(HTOK[hs][0:r, :], XIN[sl][0:r, :], AF.Square, accum_out=st(c, 0, r)),
                 reads=[B_xin[sl]], writes=[B_htok[hs], B_stat[c]])
            S.op("pool", TT(st(c, 1, r), st(c, 0, r), cc(C_DEPS, r), ALU.add), reads=[B_cst], writes=[B_stat[c]])
            S.op("pool", TT(st(c, 2, r), st(c, 1, r), cc(C_NH, r), ALU.pow), reads=[B_cst], writes=[B_stat[c]])

        def A2(c):
            r, sl, hs = crow(c), c % 3, c % 3
            S.op("dve", STT(HTOK[hs][0:r, :], XIN[sl][0:r, :], st(c, 2, r), BC0[0:r, :], ALU.mult, ALU.mult),
                 reads=[B_xin[sl], B_stat[c], B_bc0], writes=[B_htok[hs]])

        def A3(c):
            r, sl, hs = crow(c), c % 2, c % 3
            pt = bank_bf(sl)
            S.group("pe", [TR(pt[:, k * 128:k * 128 + r], HTOK[hs][0:r, k * 128:(k + 1) * 128], ident[0:r, 0:r])
                           for k in range(8)], reads=[B_htok[hs], B_identb], writes=[B_bank[sl]])

        def A4(c):
            r, t0, sl = crow(c), c * 128, c % 2
            pt = bank_bf(sl)
            S.op("act", ACT(HT3[:, :, t0:t0 + r], pt.rearrange("p (k t) -> p k t", t=128)[:, :, 0:r], AF.Copy),
                 reads=[B_bank[sl]], writes=[B_hT[c]])

        def Bmm(c):
            r, t0, sl = crow(c), c * 128, c % 2
            pb = 2 + 2 * sl
            pv = ps[pb // 2]
            for q in range(4):
                fns = [MM(pv[0:r, q * 256:(q + 1) * 256], HT[:, k * NT + t0:k * NT + t0 + r], wk(NV0 + q, k),
                          k == 0, k == 7) for k in range(8)]
                S.group("pe", fns, reads=[B_hT[c], B_ring[wslot(NV0 + q)]], writes=[B_bank[pb + q // 2]])

        def Bact_g(c):
            r, sl, gs = crow(c), c % 2, c % 4
            pb = 2 + 2 * sl
            pv = ps[pb // 2]
            gv = GV[gs]
            S.op("act", ACT(gv[0:r, :], pv[0:r, :], AF.Gelu_apprx_tanh, accum_out=st(c, 3, r)),
                 reads=[B_bank[pb], B_bank[pb + 1]], writes=[B_gv[gs], B_stat[c]])

        def Bact_s(c):
            r, gs = crow(c), c % 4
            gv = GV[gs]
            S.op("act", ACT(JUNK[0:r, :], gv[0:r, :], AF.Square, accum_out=st(c, 4, r)),
                 reads=[B_gv[gs]], writes=[B_junk, B_stat[c]])

        def C1(c):
            r = crow(c)
            S.op("dve", TS(st(c, 5, r), st(c, 3, r), INVD, None, ALU.mult), writes=[B_stat[c]])
            S.op("dve", TS(st(c, 6, r), st(c, 5, r), st(c, 5, r), -LN_EPS, ALU.mult, ALU.add), writes=[B_stat[c]])
            S.op("dve", STT(st(c, 9, r), st(c, 4, r), INVD, st(c, 6, r), ALU.mult, ALU.subtract), writes=[B_stat[c]])

        def C2(c):
            r = crow(c)
            S.op("pool", TT(st(c, 10, r), st(c, 9, r), cc(C_NH, r), ALU.pow), reads=[B_cst], writes=[B_stat[c]])

        BC1b, BC2b = YA[:, 6144:7168], YA[:, 7168:8192]
        N16 = [YA[:, 8192:9216], YA[:, 9216:10240]]
        B_bc1b, B_bc2b = Buf("bc1b"), Buf("bc2b")
        B_n16 = [Buf("n16_0"), Buf("n16_1")]
        S.op("dve", CP(BC1b, BC1), reads=[B_bc1], writes=[B_bc1b])
        S.op("dve", CP(BC2b, BC2), reads=[B_bc2], writes=[B_bc2b])

        def C3(c):
            r, gs = crow(c), c % 4
            gv = GV[gs]
            va = VA[0:r, c * D:(c + 1) * D]
            S.op("dve", STT(st(c, 12, r), st(c, 5, r), -1.0, st(c, 10, r), ALU.mult, ALU.mult), writes=[B_stat[c]])
            if c < 15:
                n16 = N16[c % 2]
                S.op("dve", TS(n16[0:r, :], gv[0:r, :], st(c, 10, r), st(c, 12, r), ALU.mult, ALU.add),
                     reads=[B_stat[c], B_gv[gs]], writes=[B_n16[c % 2]])
                S.op("dve", TT(n16[0:r, :], n16[0:r, :], BC1b[0:r, :], ALU.mult), reads=[B_bc1b], writes=[B_n16[c % 2]])
                S.op("dve", TT(va, n16[0:r, :], BC2b[0:r, :], ALU.add), reads=[B_n16[c % 2], B_bc2b], writes=[B_va[c]])
            else:
                S.op("dve", TS(gv[0:r, :], gv[0:r, :], st(c, 10, r), st(c, 12, r), ALU.mult, ALU.add),
                     reads=[B_stat[c]], writes=[B_gv[gs]])
                S.op("dve", TT(gv[0:r, :], gv[0:r, :], BC1[0:r, :], ALU.mult), reads=[B_bc1], writes=[B_gv[gs]])
                S.op("dve", TT(gv[0:r, :], gv[0:r, :], BC2[0:r, :], ALU.add), reads=[B_bc2], writes=[B_gv[gs]])
                dst = vp_d if c == 15 else vs_d
                ob = Buf("vout%d" % c)
                S.dma("sp", [DMA(dst[:, :], gv[0:r, :])], ob, reads=[B_gv[gs]])
                out_bufs.append(ob)
                S.op("dve", CP(va, gv[0:r, :]), reads=[B_gv[gs]], writes=[B_va[c]])

        def okc(c):
            return 0 <= c < NCH
        ph1 = stop_after >= 1
        def emit_blk_dmas():
            S.dma("sp", [DMA(blk3[4 * q:4 * q + 4, :, 4 * q:4 * q + 4], wT3[0:4, :, 0:4]) for q in range(16)],
                  B_blk, reads=[B_wT], writes=[B_blk])
            B_blk2 = Buf("blk2")
            S.dma("sp", [DMA(blk[64:65, :], bsr_s[0:1, :])], B_blk2, reads=[B_bsrs], writes=[B_blk2])
            return B_blk2

        LAG = 0
        for i in range(NCH + 6 + LAG):
            if okc(i - 5 - LAG) and ph1:
                C1(i - 5 - LAG)
            if okc(i):
                A0(i)
            if okc(i - 5 - LAG) and ph1:
                C2(i - 5 - LAG)
            if okc(i - 1):
                A2(i - 1)
            if okc(i - 2):
                A3(i - 2)
            if okc(i - 3 - LAG) and ph1:
                Bmm(i - 3 - LAG)
            if okc(i - 4 - LAG) and ph1:
                Bact_g(i - 4 - LAG)
            if okc(i - 2):
                A4(i - 2)
            if okc(i - 4 - LAG) and ph1:
                Bact_s(i - 4 - LAG)
            if okc(i - 5 - LAG) and ph1:
                C3(i - 5 - LAG)
            if okc(i - 1):
                if okc(i + 1) and i + 1 >= 2:
                    A_load(i + 1)
            if i == 6:
                issue_loads(NRING)
            if i == 16:
                B_blk2 = emit_blk_dmas()

        ring_state = {"rr": 0, "ss": 0}

        def alloc_bank():
            b = ring_state["rr"] % 8
            ring_state["rr"] += 1
            return b

        def alloc_out(g):
            b = alloc_bank()
            return bank(b)[:, 0:GROUPS[g][1]], B_bank[b]

        def hT_bufs(g):
            return [B_hT[c] for (c, _, _, _) in group_chunks(g)]

        def proj(blk_i, c0, g, SRC, src_bufs):
            t0, n = GROUPS[g]
            o, ob = alloc_out(g)
            fns = [MM(o, wk(blk_i, k, c0, 128), SRC[:, k * NT + t0:k * NT + t0 + n], k == 0, k == 7) for k in range(8)]
            S.group("pe", fns, reads=list(src_bufs) + [B_ring[wslot(blk_i)]], writes=[ob])
            return o, ob

        def proj_hT(blk_i, c0, g):
            return proj(blk_i, c0, g, HT, hT_bufs(g))

        T1 = SCR[:, 0:NT]
        T2 = [SCR[:, NT:NT + 512], SCR[:, NT + 512:NT + 1024]]
        B_t1 = [Buf("t1_%d" % g) for g in range(5)]
        B_t2 = [Buf("t2_0"), Buf("t2_1")]
        for b in B_t1 + B_t2:
            b.inherit(B_junk)
        for b in B_ya:
            b.inherit(B_gv[2], B_gv[3], B_xin[2], B_bc1b, B_bc2b, B_ws, B_sttok, B_htok[2], *B_n16)

        if stop_after >= 2:
            NU0 = 4
            B_brow = [Buf("brow0"), Buf("brow1")]
            for h in range(8):
                hb, hc = h // 2, (h % 2) * 128
                bu, bg = NU0 + 2 * hb, NU0 + 2 * hb + 1
                if h % 2 == 0:
                    issue_loads(bg + 1 + NRING - 2)
                S.group("act", [ACT(brow[0:1, (h % 2) * 512 + 128 * r_:(h % 2) * 512 + 128 * (r_ + 1)],
                                    bsr[0:1, h * 128:(h + 1) * 128], AF.Copy) for r_ in range(4)],
                        reads=[B_bsr], writes=[B_brow[h % 2]])
                for g in range(5):
                    t0, n = GROUPS[g]
                    o, ob = proj_hT(bu, hc, g)
                    S.op("act", ACT(T1[:, t0:t0 + n], o, AF.Gelu_apprx_tanh), reads=[ob], writes=[B_t1[g]])
                for g in range(5):
                    t0, n = GROUPS[g]
                    sl = g % 2
                    o, ob = proj_hT(bg, hc, g)
                    S.op("act", ACT(T2[sl][:, 0:n], o, AF.Silu), reads=[ob], writes=[B_t2[sl]])
                    S.op("dve", TT(T2[sl][:, 0:n], T2[sl][:, 0:n], T1[:, t0:t0 + n], ALU.mult),
                         reads=[B_t1[g]], writes=[B_t2[sl]])
                    po, pob = alloc_out(g)
                    fns = []
                    rd = [B_brow[h % 2], B_ones, B_wT]
                    chs = group_chunks(g)
                    pch = [x for x in chs if x[0] < 16]
                    npp = sum(lb - la for (_, la, lb, _) in pch)
                    la0 = pch[0][1]
                    br = brow[0:1, (h % 2) * 512 + la0:(h % 2) * 512 + la0 + npp]
                    fns.append(MM(po[:, 0:npp], ones[0:1, :], br, True, False))
                    for (c, la, lb, off) in chs:
                        if c < 16:
                            fns.append(MM(po[:, off:off + lb - la], VA[:, c * D + h * 128:c * D + h * 128 + 128],
                                          wT[:, h * 128 + la:h * 128 + lb], False, c == pch[-1][0]))
                            rd.append(B_va[c])
                        else:
                            fns.append(MM(po[:, off:off + 64], VA[0:65, 16 * D + h * 128:16 * D + h * 128 + 128],
                                          blk[0:65, h * 64:(h + 1) * 64], True, True))
                            rd += [B_va[16], B_blk, B_blk2, B_vaone]
                    S.group("pe", fns, reads=rd, writes=[pob])
                    S.op("dve", TT(YA[:, h * NT + t0:h * NT + t0 + n], po, T2[sl][:, 0:n], ALU.mult),
                         reads=[pob, B_t2[sl]], writes=[B_ya[h]])

        YB = VA
        B_ci = [Buf("ci0"), Buf("ci1")]
        B_acc = [Buf("acc0"), Buf("acc1")]
        B_sg = [Buf("sg0"), Buf("sg1")]
        B_accs = Buf("accs")
        if stop_after >= 3:
            NC0 = 12
            for b in B_yb:
                b.inherit(*B_va)
                b.inherit(B_vaone)
            CI = [SCR[:, 0:514], SCR[:, 514:1028]]
            ACC = [SCR[:, 1028:1540], SCR[:, 1540:2052]]
            SG = [SCR[:, 2052:2564], SCR[:, 2564:3076]]
            ACCS = SCR[:, 3076:3140]
            for b in B_ci + B_acc + B_sg + [B_accs]:
                b.inherit(*(B_t1 + B_t2))
            B_csall = Buf("cs_all")
            cs3 = cs_all[:, :].rearrange("p (j m) -> p j m", m=34)
            for j in range(8):
                jb, jc = j // 2, (j % 2) * 128
                b0 = NC0 + 4 * jb
                if j % 2 == 0:
                    issue_loads(b0 + 4 + NRING - 4)
                w0, w1, w2 = (convw[:, j * 3 + k:j * 3 + k + 1] for k in range(3))
                for g in range(5):
                    t0, n = GROUPS[g]
                    sl = g % 2
                    pc, pcb = proj_hT(b0 + 0, jc, g)
                    ph, phb = proj_hT(b0 + 1, jc, g)
                    pbv, pbvb = proj_hT(b0 + 2, jc, g)
                    pg, pgb = proj_hT(b0 + 3, jc, g)
                    npr = min(t0 + n, SEQ) - t0
                    has_s = t0 + n > SEQ
                    ci = CI[sl]
                    acc = ACC[sl]
                    sg = SG[sl]
                    if g == 0:
                        S.op("pool", MS(ci[:, 0:2], 0.0), writes=[B_ci[sl]])
                    else:
                        pn = min(GROUPS[g - 1][0] + GROUPS[g - 1][1], SEQ) - GROUPS[g - 1][0]
                        S.op("pool", CP(ci[:, 0:2], CI[1 - sl][:, pn:pn + 2]), reads=[B_ci[1 - sl]], writes=[B_ci[sl]])
                    S.op("act", ACT(ci[:, 2:2 + npr], pc[:, 0:npr], AF.Copy), reads=[pcb], writes=[B_ci[sl]])
                    S.op("dve", TT(ci[:, 2:2 + npr], ci[:, 2:2 + npr], ph[:, 0:npr], ALU.mult), reads=[phb], writes=[B_ci[sl]])
                    S.op("act", ACT(acc[:, 0:npr], ci[:, 2:2 + npr], AF.Identity, scale=w2), reads=[B_ci[sl], B_convw],
                         writes=[B_acc[sl]])
                    S.op("dve", STT(acc[:, 0:npr], ci[:, 1:1 + npr], w1, acc[:, 0:npr], ALU.mult, ALU.add),
                         reads=[B_ci[sl], B_convw], writes=[B_acc[sl]])
                    S.op("dve", STT(acc[:, 0:npr], ci[:, 0:npr], w0, acc[:, 0:npr], ALU.mult, ALU.add),
                         reads=[B_ci[sl], B_convw], writes=[B_acc[sl]])
                    if g == 4:
                        S.op("pool", CP(cs3[:, j, 0:2], ci[:, npr:npr + 2]), reads=[B_ci[sl]], writes=[B_csall])
                    if has_s:
                        cj = ci_s4[:, j, :, :]
                        pcs_ = pc[:, npr:npr + 64].rearrange("p (q t) -> p q t", t=4)
                        phs_ = ph[:, npr:npr + 64].rearrange("p (q t) -> p q t", t=4)
                        S.op("act", ACT(cj[:, :, 2:6], pcs_, AF.Copy), reads=[pcb], writes=[B_cis[j]])
                        S.op("dve", TT(cj[:, :, 2:6], cj[:, :, 2:6], phs_, ALU.mult), reads=[phb], writes=[B_cis[j]])
                        acc3 = ACCS.rearrange("p (q t) -> p q t", t=4)
                        S.op("act", ACT(acc3, cj[:, :, 2:6], AF.Identity, scale=w2), reads=[B_cis[j], B_convw], writes=[B_accs])
                        S.op("dve", STT(acc3, cj[:, :, 1:5], w1, acc3, ALU.mult, ALU.add), reads=[B_cis[j], B_convw],
                             writes=[B_accs])
                        S.op("dve", STT(acc3, cj[:, :, 0:4], w0, acc3, ALU.mult, ALU.add), reads=[B_cis[j], B_convw],
                             writes=[B_accs])
                        S.op("pool", CP(cs3[:, j, 2:34].rearrange("p (q r) -> p q r", r=2), cj[:, :, 4:6]),
                             reads=[B_cis[j]], writes=[B_csall])
                    S.op("act", ACT(sg[:, 0:n], pg, AF.Silu), reads=[pgb], writes=[B_sg[sl]])
                    S.op("dve", TT(acc[:, 0:npr], acc[:, 0:npr], pbv[:, 0:npr], ALU.mult), reads=[pbvb], writes=[B_acc[sl]])
                    S.op("dve", TT(YB[:, j * NT + t0:j * NT + t0 + npr], acc[:, 0:npr], sg[:, 0:npr], ALU.mult),
                         reads=[B_acc[sl], B_sg[sl]], writes=[B_yb[j]])
                    if has_s:
                        S.op("dve", TT(ACCS, ACCS, pbv[:, npr:npr + 64], ALU.mult), reads=[pbvb], writes=[B_accs])
                        S.op("dve", TT(YB[:, j * NT + SEQ:j * NT + SEQ + 64], ACCS, sg[:, npr:npr + 64], ALU.mult),
                             reads=[B_accs, B_sg[sl]], writes=[B_yb[j]])
            if True:
                csb = alloc_bank()
                while csb % 2 != 0:
                    csb = alloc_bank()
                alloc_bank()
                pcs = ps[csb // 2]
                S.group("pe", [TR(pcs[0:34, j * 128:(j + 1) * 128], cs3[:, j, :], identf[:]) for j in range(8)],
                        reads=[B_csall, B_id], writes=[B_bank[csb], B_bank[csb + 1]])
                CSO = SCR[0:34, 3200:4224]
                B_cso = Buf("cso")
                B_cso.inherit(*(B_t1 + B_t2))
                S.op("act", ACT(CSO, pcs[0:34, :], AF.Copy), reads=[B_bank[csb], B_bank[csb + 1]], writes=[B_cso])
                S.dma("sp", [DMA(csp_d[:, :], SCR[0:2, 3200:4224]), DMA(css_d[:, :], SCR[2:34, 3200:4224])],
                      B_cso, reads=[B_cso])
                out_bufs.append(B_cso)

        if stop_after >= 4:
            NM0 = 28
            for b in B_mg_all:
                b.inherit(*(B_xin + B_gv + B_htok + [B_bc0, B_bc1, B_bc2, B_ws, B_sttok]))
            TA = [SCR[:, 0:512], SCR[:, 512:1024]]
            TB = [SCR[:, 1024:1536], SCR[:, 1536:2048]]
            PP = [SCR[:, 2048:2560], SCR[:, 2560:3072]]
            B_ta = [Buf("ta0"), Buf("ta1")]
            B_tb = [Buf("tb0"), Buf("tb1")]
            B_pp = [Buf("pp0"), Buf("pp1")]
            for b in B_ta + B_tb + B_pp:
                b.inherit(*(B_ci + B_acc + B_sg + [B_accs] + B_t1 + B_t2))
            for j in range(8):
                jb, jc = j // 2, (j % 2) * 128
                b0 = NM0 + 4 * jb
                if j % 2 == 0:
                    issue_loads(b0 + 4 + NRING - 4)
                for g in range(5):
                    t0, n = GROUPS[g]
                    sl = g % 2
                    pma, pmab = proj_hT(b0 + 0, jc, g)
                    pmb, pmbb = proj_hT(b0 + 1, jc, g)
                    pa, pab = proj(b0 + 2, jc, g, YA, B_ya)
                    pbq, pbqb = proj(b0 + 3, jc, g, YB, B_yb)
                    S.op("act", ACT(TA[sl][:, 0:n], pma, AF.Tanh, scale=0.5), reads=[pmab], writes=[B_ta[sl]])
                    S.op("act", ACT(TB[sl][:, 0:n], pmb, AF.Tanh, scale=0.5), reads=[pmbb], writes=[B_tb[sl]])
                    S.op("dve", STT(PP[sl][:, 0:n], TA[sl][:, 0:n], 1.0, pa, ALU.add, ALU.mult),
                         reads=[B_ta[sl], pab], writes=[B_pp[sl]])
                    S.op("dve", STT(TB[sl][:, 0:n], TB[sl][:, 0:n], 1.0, pbq, ALU.add, ALU.mult),
                         reads=[pbqb], writes=[B_tb[sl]])
                    S.op("dve", TT(MG[:, j * NT + t0:j * NT + t0 + n], PP[sl][:, 0:n], TB[sl][:, 0:n], ALU.add),
                         reads=[B_pp[sl], B_tb[sl]], writes=[B_mg[j][g]])

        if stop_after >= 5:
            NO0, NG0 = 44, 48
            issue_loads(NBLK)
            dead = B_hT + B_ya + B_yb + B_va
            HTf = HT[:, :].bitcast(F32)
            YAf = YA[:, :].bitcast(F32)
            NXO = 5
            XO = [HTf[:, i * 1024:(i + 1) * 1024] for i in range(NXO)]
            TG = [HTf[:, 5120:6144], HTf[:, 6144:7168]]
            BCP = HTf[:, 7168:8192]
            YO = [YAf[:, 0:1024], YAf[:, 1024:2048]]
            BCF = YAf[:, 2048:3072]
            PIN = [YAf[:, 3072 + 256 * i:3072 + 256 * (i + 1)] for i in range(3)]
            o = 7680
            PB16 = [YA[:, o:o + 256], YA[:, o + 256:o + 512]]
            HN = [YA[:, o + 512:o + 1536], YA[:, o + 1536:o + 2560]]
            HNT = [YA[:, o + 2560:o + 3584], YA[:, o + 3584:o + 4608]]
            PT = [YA[:, o + 4608:o + 4864], YA[:, o + 4864:o + 5120]]
            JK = YA[:, o + 5120:o + 6144]
            B_xo = [Buf("xo%d" % i) for i in range(NXO)]
            B_tg = [Buf("tg0"), Buf("tg1")]
            B_yo = [Buf("yo0"), Buf("yo1")]
            B_bcp, B_bcf = Buf("bcp"), Buf("bcf")
            B_pin = [Buf("pin%d" % i) for i in range(3)]
            B_pb16 = [Buf("pb0"), Buf("pb1")]
            B_hn = [Buf("hn0"), Buf("hn1")]
            B_hnt = [Buf("hnt0"), Buf("hnt1")]
            B_pt = [Buf("pt0"), Buf("pt1")]
            B_jk = Buf("jk")
            B_st5 = [Buf("st5_%d" % c) for c in range(NCH)]
            for b in B_xo + B_tg + B_yo + [B_bcp, B_bcf] + B_pin + B_pb16 + B_hn + B_hnt + B_pt + [B_jk]:
                b.inherit(*dead)
            for b in B_st5:
                b.inherit(*B_stat)
            S.dma("sp", [DMA(BCP, pe_g_d.partition_broadcast(128))], B_bcp, writes=[B_bcp])
            S.dma("sp", [DMA(BCF, fin_g_d.partition_broadcast(128))], B_bcf, writes=[B_bcf])
            S.op("dve", TS(BCP, BCP, 16.0, None, ALU.mult), writes=[B_bcp])
            S.op("dve", TS(BCF, BCF, 32.0, None, ALU.mult), writes=[B_bcf])

            def load5(c):
                r, t0 = crow(c), c * 128
                S.dma("sp", [DMA(XO[c % NXO][0:r, :], x_d[t0:t0 + r, :])], B_xo[c % NXO], writes=[B_xo[c % NXO]])
                S.dma("sp", [DMA(PIN[c % 3][0:r, :], p_d[t0:t0 + r, :])], B_pin[c % 3], writes=[B_pin[c % 3]])

            pA, pG, pQ = ps[0], ps[1], ps[2]

            def s_wo(c):
                r, t0 = crow(c), c * 128
                fns = []
                for q in range(4):
                    for k in range(8):
                        fns.append(MM(pA[0:r, q * 256:(q + 1) * 256], MG[:, k * NT + t0:k * NT + t0 + r], wk(NO0 + q, k),
                                      k == 0, k == 7))
                gs_ = [g for g in range(5) if any(cc_ == c for (cc_, _, _, _) in group_chunks(g))]
                S.group("pe", fns, reads=[B_mg[j][g] for j in range(8) for g in gs_] + [B_ring[wslot(NO0 + q)] for q in range(4)],
                        writes=[B_bank[0], B_bank[1]])

            def s_sqf(c):
                r = crow(c)
                S.op("act", ACT(JK[0:r, :], XO[c % NXO][0:r, :], AF.Square, accum_out=st(c, 10, r)),
                     reads=[B_xo[c % NXO]], writes=[B_jk, B_st5[c]])

            def s_pb16(c):
                r = crow(c)
                S.op("pool", CP(PB16[c % 2][0:r, :], PIN[c % 3][0:r, :]), reads=[B_pin[c % 3]], writes=[B_pb16[c % 2]])

            def s_r3(c):
                r = crow(c)
                S.op("pool", TT(st(c, 11, r), st(c, 10, r), cc(C_DEPS, r), ALU.add), reads=[B_cst], writes=[B_st5[c]])
                S.op("pool", TT(st(c, 12, r), st(c, 11, r), cc(C_NH, r), ALU.pow), reads=[B_cst], writes=[B_st5[c]])

            def s_xo(c):
                r = crow(c)
                xo = XO[c % NXO]
                S.op("dve", STT(xo[0:r, :], pA[0:r, :], 0.5, xo[0:r, :], ALU.mult, ALU.add),
                     reads=[B_bank[0], B_bank[1]], writes=[B_xo[c % NXO]])

            def s_sqa(c):
                r = crow(c)
                S.op("act", ACT(JK[0:r, :], XO[c % NXO][0:r, :], AF.Square, accum_out=st(c, 13, r)),
                     reads=[B_xo[c % NXO]], writes=[B_jk, B_st5[c]])

            def s_hn(c):
                r = crow(c)
                S.op("pool", TT(HN[c % 2][0:r, :], XO[c % NXO][0:r, :], BCP[0:r, :], ALU.mult),
                     reads=[B_xo[c % NXO], B_bcp], writes=[B_hn[c % 2]])

            def s_r2(c):
                r = crow(c)
                S.op("pool", TT(st(c, 14, r), st(c, 13, r), cc(C_DEPS, r), ALU.add), reads=[B_cst], writes=[B_st5[c]])
                S.op("pool", TT(st(c, 15, r), st(c, 14, r), cc(C_NH, r), ALU.pow), reads=[B_cst], writes=[B_st5[c]])

            def s_y(c):
                r, t0 = crow(c), c * 128
                yo = YO[c % 2]
                S.op("dve", STT(yo[0:r, :], XO[c % NXO][0:r, :], st(c, 12, r), BCF[0:r, :], ALU.mult, ALU.mult),
                     reads=[B_xo[c % NXO], B_st5[c], B_bcf], writes=[B_yo[c % 2]])
                S.dma("sp", [DMA(y_d[t0:t0 + r, :], yo[0:r, :])], B_yo[c % 2], reads=[B_yo[c % 2]])

            def s_wpg(c):
                r = crow(c)
                sl = c % 2
                fns = []
                for q in range(4):
                    for k in range(8):
                        fns.append(MM(pG[0:r, q * 256:(q + 1) * 256], HNT[sl][:, k * 128:k * 128 + r], wk(NG0 + q, k),
                                      k == 0, k == 7))
                S.group("pe", fns, reads=[B_hnt[sl]] + [B_ring[wslot(NG0 + q)] for q in range(4)],
                        writes=[B_bank[2], B_bank[3]])
                fns = []
                for q in range(4):
                    for k in range(2):
                        fns.append(MM(pQ[0:r, q * 256:(q + 1) * 256], PT[sl][:, k * 128:k * 128 + r],
                                      WPP[:, k * D + q * 256:k * D + q * 256 + 256], k == 0, k == 1))
                S.group("pe", fns, reads=[B_pt[sl], B_wpp], writes=[B_bank[4], B_bank[5]])

            def s_tanh(c):
                r = crow(c)
                S.op("act", ACT(TG[c % 2][0:r, :], pG[0:r, :], AF.Tanh, scale=st(c, 15, r)),
                     reads=[B_bank[2], B_bank[3], B_st5[c]], writes=[B_tg[c % 2]])

            def s_qxf(c):
                r = crow(c)
                tg, xo = TG[c % 2], XO[c % NXO]
                S.op("dve", STT(tg[0:r, :], tg[0:r, :], 1.0, pQ[0:r, :], ALU.add, ALU.mult),
                     reads=[B_bank[4], B_bank[5]], writes=[B_tg[c % 2]])
                S.op("dve", STT(xo[0:r, :], tg[0:r, :], 0.5, xo[0:r, :], ALU.mult, ALU.add),
                     reads=[B_tg[c % 2]], writes=[B_xo[c % NXO]])

            def s_tr(c):
                r = crow(c)
                sl = c % 2
                pt, pt2 = bank_bf(6), bank_bf(7)
                S.group("pe", [TR(pt[:, k * 128:k * 128 + r], HN[sl][0:r, k * 128:(k + 1) * 128], ident[0:r, 0:r])
                               for k in range(8)], reads=[B_hn[sl], B_identb], writes=[B_bank[6]])
                S.group("pe", [TR(pt2[:, k * 128:k * 128 + r], PB16[sl][0:r, k * 128:(k + 1) * 128], ident[0:r, 0:r])
                               for k in range(2)], reads=[B_pb16[sl], B_identb], writes=[B_bank[7]])

            def s_hnt(c):
                r = crow(c)
                sl = c % 2
                pt = bank_bf(6)
                S.op("act", ACT(HNT[sl].rearrange("p (k t) -> p k t", t=128)[:, :, 0:r],
                                pt.rearrange("p (k t) -> p k t", t=128)[:, :, 0:r], AF.Copy),
                     reads=[B_bank[6]], writes=[B_hnt[sl]])

            def s_ptc(c):
                r = crow(c)
                sl = c % 2
                pt2 = bank_bf(7)
                S.op("dve", CP(PT[sl].rearrange("p (k t) -> p k t", t=128)[:, :, 0:r],
                               pt2[:, 0:256].rearrange("p (k t) -> p k t", t=128)[:, :, 0:r]),
                     reads=[B_bank[7]], writes=[B_pt[sl]])

            load5(0)
            load5(1)

            def ok(c):
                return 0 <= c < NCH
            for i in range(NCH + 2):
                if ok(i):
                    s_wo(i)
                if ok(i - 2):
                    s_sqf(i - 2)
                if ok(i):
                    s_pb16(i)
                if ok(i - 2):
                    s_r3(i - 2)
                if ok(i):
                    s_xo(i)
                    s_sqa(i)
                    s_hn(i)
                    s_r2(i)
                if ok(i - 2):
                    s_y(i - 2)
                if ok(i - 1):
                    s_wpg(i - 1)
                    s_tanh(i - 1)
                    s_qxf(i - 1)
                if ok(i):
                    s_tr(i)
                    s_hnt(i)
                    s_ptc(i)
                if ok(i + 2):
                    load5(i + 2)
            out_bufs += B_yo

        def tap(name, src, bufs):
            if name in dbg_d:
                bd = Buf("dbg_" + name)
                S.dma("sp", [DMA(dbg_d[name][:, :], src)], bd, reads=bufs)
                out_bufs.append(bd)
        tap("d_hT", HT[:, :], B_hT)
        tap("d_va", VA[:, :], B_va + B_yb)
        tap("d_ya", YA[:, :], B_ya)
        tap("d_mg", MG[:, :], B_mg_all)

        S.wait_all("sp", out_bufs)
        with nc.Block() as block:
            S.emit(block)
    return nc


def make_in_maps(inputs, n_cores=N_CORES):
    f = lambda a: np.ascontiguousarray(np.asarray(a, dtype=np.float32))
    xp, xs = f(inputs["x_prompt"]), f(inputs["x_sample"])
    pp, psm = f(inputs["p_prompt"])[0], f(inputs["p_sample"])[0]
    stc = f(inputs["state_conv"])[0]
    shared = {
        "w_in": f(inputs["w_in"])[0], "w_a": f(inputs["w_a_out"])[0], "w_b": f(inputs["w_b_out"])[0],
        "w_o": f(inputs["w_o"])[0], "w_pg": f(inputs["w_pe_gate"])[0], "w_pp": f(inputs["w_pe_proj"])[0],
        "norm_g": f(inputs["norm_g"])[0], "ln_g": f(inputs["ln_v_g"])[0], "ln_b": f(inputs["ln_v_b"])[0],
        "pe_g": f(inputs["pe_norm_g"])[0], "fin_g": f(inputs["final_norm_g"]),
        "w_s": f(inputs["w_s"])[0], "b_s": f(inputs["b_s"])[0], "conv_w": f(inputs["conv_w"])[0],
    }
    maps = []
    for c in range(n_cores):
        m = dict(shared)
        m["x"] = np.ascontiguousarray(np.concatenate([xp[c], xs[16 * c:16 * c + 16].reshape(NS, D)], axis=0))
        m["p"] = np.ascontiguousarray(np.concatenate([pp[c], psm[16 * c:16 * c + 16].reshape(NS, P_DIM)], axis=0))
        m["st"] = np.ascontiguousarray(stc[16 * c:16 * c + 16].reshape(32, D))
        maps.append(m)
    return maps


_NC_CACHE = {}


def kernel(**inputs):
    if "nc" not in _NC_CACHE:
        _NC_CACHE["nc"] = build_program()
    nc = _NC_CACHE["nc"]
    maps = make_in_maps(inputs)
    res = run_bass_kernel_spmd(nc, maps, core_ids=list(range(N_CORES)))
    R = res.results
    y_prompt = np.stack([R[c]["y"][:SEQ] for c in range(N_CORES)], axis=0)
    y_sample = np.concatenate([R[c]["y"][SEQ:].reshape(16, 4, D) for c in range(N_CORES)], axis=0)
    cs_p = np.stack([R[c]["csp"] for c in range(N_CORES)], axis=0)[None]
    cs_s = np.concatenate([R[c]["css"].reshape(16, 2, D) for c in range(N_CORES)], axis=0)[None]
    v_p = np.stack([R[c]["vp"] for c in range(N_CORES)], axis=0)[None]
    v_s = np.concatenate([R[c]["vs"].reshape(16, 4, D) for c in range(N_CORES)], axis=0)[None]
    return (y_prompt.astype(np.float32), y_sample.astype(np.float32), cs_p.astype(np.float32),
            cs_s.astype(np.float32), v_p.astype(np.float32), v_s.astype(np.float32))
```
